# Optimizing a Trainium2 kernel written in Bass

```python
import jax, jax.numpy as jnp
from jax import lax
import numpy as np

D_MODEL = 1024
BATCH = 16
SEQ = 4096
DEPTH = 1

HEAD_DIM = 64
N_HEADS_SB = 8
N_HEADS_FOX = 8
D_SB = N_HEADS_SB * HEAD_DIM
D_FOX = N_HEADS_FOX * HEAD_DIM
D_FF = -(-8 * D_MODEL // (3 * 256)) * 256
Q_BLOCK = 128
RMS_EPS = 1e-6
N_MOD = 6
FORGET_BIAS_INIT = 3.0
IN_SPLITS = (D_SB, 2 * D_SB, 3 * D_SB,
             3 * D_SB + D_FOX, 3 * D_SB + 2 * D_FOX, 3 * D_SB + 3 * D_FOX,
             3 * D_SB + 3 * D_FOX + N_HEADS_FOX,
             3 * D_SB + 3 * D_FOX + N_HEADS_FOX + D_MODEL)
D_IN_PROJ = 3 * D_SB + 3 * D_FOX + N_HEADS_FOX + 2 * D_MODEL

kernel_name = 'hybrid_stickbreak_fox_gated_block'


def rms_norm(x, g):
    xf = x.astype(jnp.float32)
    y = xf * lax.rsqrt(jnp.mean(xf * xf, axis=-1, keepdims=True) + RMS_EPS)
    return (y * g.astype(jnp.float32)).astype(x.dtype)


def to_blocks(t):
    b, s, h, d = t.shape
    return t.reshape(b, s // Q_BLOCK, Q_BLOCK, h, d).transpose(1, 0, 3, 2, 4)


def from_blocks(t):
    nb, b, h, q, d = t.shape
    return t.transpose(1, 0, 3, 2, 4).reshape(b, nb * q, h * d)


def stick_breaking_attention(q, k, v):
    s = q.shape[1]
    nb = s // Q_BLOCK
    kh = k.transpose(0, 2, 1, 3).astype(jnp.float32)
    vh = v.transpose(0, 2, 1, 3).astype(jnp.float32)
    key_pos = jnp.arange(s, dtype=jnp.int32)
    scale = HEAD_DIM ** -0.5

    def block(args):
        q_blk, start = args
        z = jnp.einsum('bhqd,bhkd->bhqk', q_blk.astype(jnp.float32), kh) * scale
        q_pos = start + jnp.arange(Q_BLOCK, dtype=jnp.int32)
        past = key_pos[None, :] < q_pos[:, None]
        log_keep = jnp.where(past, jax.nn.log_sigmoid(-z), 0.0)
        after = lax.cumsum(log_keep, axis=3, reverse=True) - log_keep
        w = jnp.where(past, jnp.exp(jax.nn.log_sigmoid(z) + after), 0.0)
        return jnp.einsum('bhqk,bhkd->bhqd', w, vh)

    starts = jnp.arange(nb, dtype=jnp.int32) * Q_BLOCK
    out = lax.map(block, (to_blocks(q), starts))
    return from_blocks(out).astype(q.dtype)


def forgetting_attention(q, k, v, log_f):
    b, s, h, _ = q.shape
    nb = s // Q_BLOCK
    cum = jnp.cumsum(log_f.astype(jnp.float32), axis=1)
    cum_k = cum.transpose(0, 2, 1)
    cum_q = cum.reshape(b, nb, Q_BLOCK, h).transpose(1, 0, 3, 2)
    kh = k.transpose(0, 2, 1, 3).astype(jnp.float32)
    vh = v.transpose(0, 2, 1, 3).astype(jnp.float32)
    key_pos = jnp.arange(s, dtype=jnp.int32)
    scale = HEAD_DIM ** -0.5

    def block(args):
        q_blk, cq, start = args
        logits = jnp.einsum('bhqd,bhkd->bhqk', q_blk.astype(jnp.float32), kh) * scale
        logits = logits + cq[..., :, None] - cum_k[:, :, None, :]
        q_pos = start + jnp.arange(Q_BLOCK, dtype=jnp.int32)
        causal = key_pos[None, :] <= q_pos[:, None]
        p = jax.nn.softmax(jnp.where(causal, logits, -jnp.inf), axis=-1)
        return jnp.einsum('bhqk,bhkd->bhqd', p, vh)

    starts = jnp.arange(nb, dtype=jnp.int32) * Q_BLOCK
    out = lax.map(block, (to_blocks(q), cum_q, starts))
    return from_blocks(out).astype(q.dtype)


def setup_inputs(seed: int = 0) -> dict:
    key = jax.random.key(seed)
    ks = jax.random.split(key, 17)
    f32 = jnp.float32

    def nrm(k, shape, fan_in):
        return jax.random.normal(k, shape, f32) * fan_in ** -0.5

    def gain(k, shape):
        return 1.0 + 0.05 * jax.random.normal(k, shape, f32)

    return {
        'x': jax.random.normal(ks[0], (BATCH, SEQ, D_MODEL), f32),
        'c': jax.random.normal(ks[1], (BATCH, D_MODEL), f32),
        'w_ada': nrm(ks[2], (DEPTH, D_MODEL, N_MOD * D_MODEL), D_MODEL),
        'b_ada': 0.02 * jax.random.normal(ks[3], (DEPTH, N_MOD * D_MODEL), f32),
        'g_mix': gain(ks[4], (DEPTH, D_MODEL)),
        'w_in': nrm(ks[5], (DEPTH, D_MODEL, D_IN_PROJ), D_MODEL),
        'b_forget': FORGET_BIAS_INIT + 0.1 * jax.random.normal(ks[6], (DEPTH, N_HEADS_FOX), f32),
        'b_gate': 0.02 * jax.random.normal(ks[7], (DEPTH, 2 * D_MODEL), f32),
        'w_branch_sb': nrm(ks[8], (DEPTH, D_SB, D_MODEL), D_SB),
        'w_branch_fox': nrm(ks[9], (DEPTH, D_FOX, D_MODEL), D_FOX),
        'w_out': nrm(ks[10], (DEPTH, D_MODEL, D_MODEL), D_MODEL),
        'g_ffn': gain(ks[11], (DEPTH, D_MODEL)),
        'w_ffn_gate': nrm(ks[12], (DEPTH, D_MODEL, D_FF), D_MODEL),
        'w_ffn_up': nrm(ks[13], (DEPTH, D_MODEL, D_FF), D_MODEL),
        'w_ffn_down': nrm(ks[14], (DEPTH, D_FF, D_MODEL), D_FF),
        'g_final': gain(ks[15], (D_MODEL,)),
    }


def reference(x, c, w_ada, b_ada, g_mix, w_in, b_forget, b_gate, w_branch_sb, w_branch_fox,
              w_out, g_ffn, w_ffn_gate, w_ffn_up, w_ffn_down, g_final):
    b, s, _ = x.shape
    c_act = jax.nn.silu(c)
    for l in range(DEPTH):
        mod = c_act @ w_ada[l] + b_ada[l]
        shift1, scale1, gate1, shift2, scale2, gate2 = [m[:, None, :] for m in jnp.split(mod, N_MOD, axis=-1)]

        h = rms_norm(x, g_mix[l]) * (1.0 + scale1) + shift1
        proj = h @ w_in[l]
        q_sb, k_sb, v_sb, q_fx, k_fx, v_fx, f_logit, gl_sb, gl_fx = jnp.split(proj, IN_SPLITS, axis=-1)
        heads_sb = lambda t: t.reshape(b, s, N_HEADS_SB, HEAD_DIM)
        heads_fx = lambda t: t.reshape(b, s, N_HEADS_FOX, HEAD_DIM)

        y_sb = stick_breaking_attention(heads_sb(q_sb), heads_sb(k_sb), heads_sb(v_sb))
        log_f = jax.nn.log_sigmoid(f_logit.astype(jnp.float32) + b_forget[l])
        y_fx = forgetting_attention(heads_fx(q_fx), heads_fx(k_fx), heads_fx(v_fx), log_f)

        gates = jax.nn.sigmoid(jnp.concatenate([gl_sb, gl_fx], axis=-1) + b_gate[l])
        g_sb, g_fx = jnp.split(gates, 2, axis=-1)
        merged = g_sb * (y_sb @ w_branch_sb[l]) + g_fx * (y_fx @ w_branch_fox[l])
        x = x + gate1 * (merged @ w_out[l])

        h2 = rms_norm(x, g_ffn[l]) * (1.0 + scale2) + shift2
        ffn = (jax.nn.silu(h2 @ w_ffn_gate[l]) * (h2 @ w_ffn_up[l])) @ w_ffn_down[l]
        x = x + gate2 * ffn
    return rms_norm(x, g_final)
```

```python
import numpy as np
from contextlib import ExitStack
import concourse.bass as bass
import concourse.mybir as mybir
from concourse.bass_utils import run_bass_kernel_spmd

F32 = mybir.dt.float32
BF16 = mybir.dt.bfloat16
AF = mybir.ActivationFunctionType
ALU = mybir.AluOpType

SEM_ROLL = 30000
SAME_ENGINE_SYNC = True

D = 1024
S = 4096
NSEQ = 2
CH = 512
NCH = S // CH
DFF = 2816
NJ = DFF // 128
EPS = 1e-6
NCORES = 8
NSLOT = 3
SLOT_ELEMS = 2048
LOOKAHEAD = NSLOT - 1
NF_SB = 4
NF_FX = 0


class Buf:
    __slots__ = ("name", "w", "r", "dsem_in", "dcnt_in", "dsem_out", "dcnt_out")

    def __init__(self, name):
        self.name = name
        self.w = {}
        self.r = {}
        self.dsem_in = None
        self.dcnt_in = 0
        self.dsem_out = None
        self.dcnt_out = 0


class Eng:
    def __init__(self, prog, name, is_pe=False):
        self.prog = prog
        self.name = name
        self.is_pe = is_pe
        self.ops = []
        self.count = 0
        self.sem = prog.new_sem(name)
        self.waited = {}

    def roll(self):
        if self.count >= SEM_ROLL:
            self.sem = self.prog.new_sem(self.name)
            self.count = 0


class Prog:
    def __init__(self, nc, stack):
        self.nc = nc
        self.stack = stack
        self.nsem = 0
        self.pe = Eng(self, "pe", True)
        self.act = Eng(self, "act")
        self.dve = Eng(self, "dve")
        self.pool = Eng(self, "pool")
        self.sp = Eng(self, "sp")

    def new_sem(self, name):
        self.nsem += 1
        return self.stack.enter_context(self.nc.semaphore(f"s{self.nsem}_{name}"))

    def _deps(self, eng, reads, writes):
        deps = {}
        for b in reads:
            for s, v in b.w.items():
                if deps.get(s, 0) < v:
                    deps[s] = v
        for b in writes:
            for d in (b.w, b.r):
                for s, v in d.items():
                    if deps.get(s, 0) < v:
                        deps[s] = v
        waits = []
        for s, v in deps.items():
            if s is eng.sem and (eng.is_pe or not SAME_ENGINE_SYNC):
                continue
            if eng.waited.get(s, 0) < v:
                eng.waited[s] = v
                waits.append((s, v))
        return waits

    def op(self, eng, fn, reads=(), writes=()):
        eng.roll()
        waits = self._deps(eng, reads, writes)
        eng.count += 1
        tok = (eng.sem, eng.count)
        eng.ops.append((waits, fn, tok[0], 1))
        for b in reads:
            if b.r.get(tok[0], 0) < tok[1]:
                b.r[tok[0]] = tok[1]
        for b in writes:
            b.w = {tok[0]: tok[1]}
            b.r = {}
        return tok

    def dma(self, eng, out_ap, in_ap, reads=(), writes=(), owner=None, is_store=False):
        waits = self._deps(eng, reads, writes)
        if owner is None:
            owner = writes[0] if writes else reads[0]
        if is_store:
            if owner.dsem_out is None:
                owner.dsem_out = self.new_sem("do_" + owner.name)
            owner.dcnt_out += 16
            tok = (owner.dsem_out, owner.dcnt_out)
        else:
            if owner.dsem_in is None:
                owner.dsem_in = self.new_sem("di_" + owner.name)
            owner.dcnt_in += 16
            tok = (owner.dsem_in, owner.dcnt_in)
        fn = (lambda e, o=out_ap, i=in_ap: e.dma_start(out=o, in_=i))
        eng.ops.append((waits, fn, tok[0], 16))
        for b in reads:
            if b.r.get(tok[0], 0) < tok[1]:
                b.r[tok[0]] = tok[1]
        for b in writes:
            b.w = {tok[0]: tok[1]}
            b.r = {}
        return tok

    def wait_all(self, eng, bufs):
        waits = self._deps(eng, (), bufs)
        eng.ops.append((waits, None, None, 0))

    def emit(self):
        nc = self.nc
        with nc.Block() as block:
            def run(e, eng):
                for waits, fn, sem, inc in eng.ops:
                    for s, v in waits:
                        e.wait_ge(s, v)
                    if fn is not None:
                        fn(e).then_inc(sem, inc)

            @block.tensor
            def _(e):
                run(e, self.pe)

            @block.scalar
            def _(e):
                run(e, self.act)

            @block.vector
            def _(e):
                run(e, self.dve)

            @block.gpsimd
            def _(e):
                run(e, self.pool)

            @block.sync
            def _(e):
                run(e, self.sp)


def build_nc():
    nc = bass.Bass("TRN2", target_bir_lowering=False)

    def din(name, shape):
        return nc.dram_tensor(name, shape, F32, kind="ExternalInput").ap()

    x_d = din("x", [NSEQ, S, D])
    cT_d = din("cT", [128, 8, NSEQ])
    w_ada_d = din("w_ada", [D, 6 * D])
    b_adaT_d = din("b_adaT", [128, 48])
    g_mixT_d = din("g_mixT", [128, 8])
    g_ffnT_d = din("g_ffnT", [128, 8])
    gfin_d = din("gfin_bc", [128, D])
    w_in_d = din("w_in", [D, 5128])
    bfor_d = din("bfor_bc", [128, 8])
    bgateT_d = din("b_gateT", [128, 16])
    w_bsb_d = din("w_bsb", [512, D])
    w_bfx_d = din("w_bfx", [512, D])
    w_out_d = din("w_out", [D, D])
    w_fg_d = din("w_fg", [D, DFF])
    w_fu_d = din("w_fu", [D, DFF])
    w_fd_d = din("w_fd", [DFF, D])
    out_d = nc.dram_tensor("out", [NSEQ, S, D], F32, kind="ExternalOutput").ap()

    with ExitStack() as st:
        P = Prog(nc, st)
        pe, act, dve, pool, sp = P.pe, P.act, P.dve, P.pool, P.sp

        def sb(name, shape, dt):
            return st.enter_context(nc.sbuf_tensor("sb_" + name, shape, dt))

        def ps(name):
            return st.enter_context(nc.psum_tensor(name, [128, 512], F32))

        KT = sb("KT", [128, 4, S], BF16)
        KT_B = [[Buf(f"KT{p}_{c}") for c in range(NCH)] for p in range(4)]
        V = sb("V", [128, 32, 768], BF16)
        V_B = [Buf(f"V{b}") for b in range(32)]
        YSB = sb("YSB", [128, 4, S], BF16)
        YSB_B = [[Buf(f"YSB{p}_{c}") for c in range(NCH)] for p in range(4)]
        X = sb("X", [128, 4, D], F32)
        X_B = [Buf(f"X{i}") for i in range(4)]
        HT = sb("HT", [128, 8, CH], BF16)
        HT_B = [Buf(f"HT{k}") for k in range(8)]
        AR = sb("AR", [128, 24, CH], BF16)
        AR_B = [Buf(f"AR{k}") for k in range(24)]
        WS = [sb(f"WS{i}", [128, SLOT_ELEMS], BF16) for i in range(NSLOT)]
        WS_B = [Buf(f"WS{i}") for i in range(NSLOT)]
        EE = [sb(f"ee{k}", [128, 2 * CH], F32) for k in range(2)]
        EE_B = [[Buf(f"ee{k}_{s}") for s in range(2)] for k in range(2)]
        E_ = [EE[0][:, CH * s:CH * (s + 1)] for s in range(2)]
        E_B = EE_B[0]
        XC = [EE[1][:, CH * s:CH * (s + 1)] for s in range(2)]
        XC_B = EE_B[1]
        SPP = [sb(f"spp{k}", [128, 2 * CH], BF16) for k in range(2)]
        SPP_B = [[Buf(f"spp{k}_{s}") for s in range(2)] for k in range(2)]
        WW = sb("ww", [128, 2 * CH], BF16)
        WW_B = [Buf(f"ww{s}") for s in range(2)]
        GBC = [sb(f"gbc{g}", [128, D], F32) for g in range(2)]
        GBC_B = [[Buf(f"gbc{g}_{h}") for h in range(2)] for g in range(2)]
        EE3 = [EE[0], EE[1], GBC[0]]
        EE3_B = [EE_B[0], EE_B[1], GBC_B[0]]
        gfin = sb("gfin", [128, D], F32)
        GFIN_B = Buf("gfin")
        ident = sb("ident", [128, 128], F32)
        ID_B = Buf("ident")
        ones32 = sb("ones32", [128, 128], F32)
        ON_B = Buf("ones32")
        triIN = sb("triIN", [128, 128], F32)
        TRIIN_B = Buf("triIN")
        selL = sb("selL", [128, 128], F32)
        SELL_B = Buf("selL")
        triN = sb("triN", [128, 128], BF16)
        TRIN_B = Buf("triN")
        cmpN = sb("cmpN", [128, 128], BF16)
        CMPN_B = Buf("cmpN")
        tmpb = sb("tmpb", [128, 128], BF16)
        TMPB_B = Buf("tmpb")
        Dm = sb("Dm", [128, 4, 128], F32)
        DM_B = [Buf(f"Dm{i}") for i in range(4)]
        Gm = sb("Gm", [128, 2, 128], F32)
        GM_B = [Buf(f"Gm{i}") for i in range(2)]
        ckT = sb("ckT", [128, 32, 8], F32)
        CK_B = [Buf(f"ck{b}") for b in range(32)]
        biasT = sb("biasT", [128, 8, 32], F32)
        BIAS_B = Buf("biasT")
        crefbc = sb("crefbc", [128, 8], F32)
        CREF_B = Buf("cref")
        ft = sb("ft", [128, 8], F32)
        FT_B = Buf("ft")
        fe = sb("fe", [128, 8], F32)
        FE_B = Buf("fe")
        fl = sb("fl", [128, 4, 8], F32)
        FL_B = [Buf(f"fl{i}") for i in range(4)]
        ss = sb("ss", [128, 4], F32)
        SS_B = Buf("ss")
        lnv = sb("lnv", [128, 4], F32)
        LNV_B = Buf("lnv")
        rstd = sb("rstd", [128, 4], F32)
        RSTD_B = Buf("rstd")
        cT = sb("cT", [128, 8, NSEQ], F32)
        CT_B = Buf("cT")
        cact = sb("cact", [128, 8, NSEQ], F32)
        CACT_B = Buf("cact")
        csig = sb("csig", [128, 8, NSEQ], F32)
        CSIG_B = Buf("csig")
        modraw = sb("modraw", [128, 48, NSEQ], F32)
        MODRAW_B = Buf("modraw")
        modT = sb("modT", [128, 48, NSEQ], F32)
        MODT_B = Buf("modT")
        b_adaT = sb("b_adaT", [128, 48], F32)
        BADA_B = Buf("b_adaT")
        g_mixT = sb("g_mixT", [128, 8], F32)
        GMIX_B = Buf("g_mixT")
        g_ffnT = sb("g_ffnT", [128, 8], F32)
        GFFN_B = Buf("g_ffnT")
        geff = sb("geff", [128, NSEQ, 2, 8], F32)
        GEFF_B = Buf("geff")
        bfor = sb("bfor", [128, 8], F32)
        BFOR_B = Buf("bfor")
        bgateT = sb("bgateT", [128, 16], F32)
        BGATE_B = Buf("bgateT")
        wf = sb("wf", [128, 8, 8], BF16)
        WF_B = Buf("wf")

        PP = [st.enter_context(nc.psum_tensor(f"pp{j}", [128, 2 * CH], F32)) for j in range(4)]
        PB = [PP[i // 2][:, CH * (i % 2):CH * (i % 2 + 1)] for i in range(8)]
        PB_B = [Buf(f"pb{i}") for i in range(8)]
        rot = {"i": 0}

        def next_bank():
            k = (0, 1, 2, 3)[rot["i"] % 4]
            rot["i"] += 1
            return PB[k], PB_B[k]

        def mm(out_ap, lhsT, rhs, start, stop, reads, writes, skip=False):
            if skip:
                P.op(pe, lambda e: e.matmul(out_ap, lhsT=lhsT, rhs=rhs, start=start, stop=stop,
                                            skip_group_check=True), reads=reads, writes=writes)
            else:
                P.op(pe, lambda e: e.matmul(out_ap, lhsT=lhsT, rhs=rhs, start=start, stop=stop),
                     reads=reads, writes=writes)

        def act_op(out, in_, func, reads, writes, **kw):
            P.op(act, lambda e: e.activation(out=out, in_=in_, func=func, **kw), reads=reads, writes=writes)

        def ts_op(eng, out, in0, s1, s2, op0, op1, reads, writes):
            if s2 is None:
                P.op(eng, lambda e: e.tensor_scalar(out=out, in0=in0, scalar1=s1, scalar2=None, op0=op0),
                     reads=reads, writes=writes)
            else:
                P.op(eng, lambda e: e.tensor_scalar(out=out, in0=in0, scalar1=s1, scalar2=s2, op0=op0, op1=op1),
                     reads=reads, writes=writes)

        def tt_op(eng, out, in0, in1, op, reads, writes):
            P.op(eng, lambda e: e.tensor_tensor(out=out, in0=in0, in1=in1, op=op), reads=reads, writes=writes)

        def copy_op(eng, out, in_, reads, writes):
            if eng is act:
                act_op(out, in_, AF.Copy, reads, writes)
            else:
                P.op(eng, lambda e: e.tensor_copy(out=out, in_=in_), reads=reads, writes=writes)

        evc = {"i": 0}

        def evac_engine():
            evc["i"] += 1
            return act if evc["i"] % 2 == 0 else dve

        scr_groups = {}
        cast_q = {}

        def make_piece(name, group, srcs):
            tot = sum(a * b for _, a, b in srcs)
            assert tot <= SLOT_ELEMS, (name, tot)
            t = nc.dram_tensor("scr_" + name, [128, tot], BF16, kind="Internal").ap()
            if group not in scr_groups:
                scr_groups[group] = Buf("scr_" + group)
            gb = scr_groups[group]
            off = 0
            for src, a, b in srcs:
                cast_q.setdefault(group, []).append(
                    (t[:, off:off + a * b].rearrange("p (a b) -> p a b", a=a), src, gb))
                off += a * b
            return (t, tot, gb)

        def flush_casts(groups):
            for g in groups:
                for dst, src, gb in cast_q.pop(g, []):
                    P.dma(pool, dst, src, writes=[gb], owner=gb)

        def emit_casts(n):
            for g in ("inB", "gate", "merge", "out", "gu", "down"):
                while n > 0 and cast_q.get(g):
                    dst, src, gb = cast_q[g].pop(0)
                    P.dma(pool, dst, src, writes=[gb], owner=gb)
                    n -= 1

        def kview(w, c0, c1):
            return w[:, c0:c1].rearrange("(k p) c -> p k c", p=128)

        P.op(pool, lambda e: e.memset(ones32[:], 1.0), writes=[ON_B])
        P.op(pool, lambda e: e.affine_select(out=ident[:], in_=ones32[:], pattern=[[-1, 128]],
                                             compare_op=ALU.is_equal, fill=0.0, base=0, channel_multiplier=1),
             reads=[ON_B], writes=[ID_B])
        ts_op(dve, selL[:], ones32[:], -1.0, None, ALU.mult, None, [ON_B], [SELL_B])
        P.op(pool, lambda e: e.affine_select(out=triIN[:], in_=selL[:], pattern=[[1, 128]],
                                             compare_op=ALU.is_ge, fill=0.0, base=0, channel_multiplier=-1),
             reads=[SELL_B], writes=[TRIIN_B])
        copy_op(dve, tmpb[:], selL[:], [SELL_B], [TMPB_B])
        P.op(pool, lambda e: e.affine_select(out=triN[:], in_=tmpb[:], pattern=[[-1, 128]],
                                             compare_op=ALU.is_ge, fill=0.0, base=0, channel_multiplier=1),
             reads=[TMPB_B], writes=[TRIN_B])
        tt_op(dve, cmpN[:], tmpb[:], triN[:], ALU.subtract, [TMPB_B, TRIN_B], [CMPN_B])
        P.op(pool, lambda e: e.affine_select(out=selL[:], in_=ones32[:], pattern=[[0, 128]],
                                             compare_op=ALU.is_ge, fill=0.0, base=-127, channel_multiplier=1),
             reads=[ON_B, TRIIN_B, TMPB_B], writes=[SELL_B])
        Vv = V[:].rearrange("p k (q c) -> p k q c", q=4)
        P.op(pool, lambda e: e.memset(Vv[:, :, :, 64:128], 1.0), writes=V_B)

        pieces = {}
        for nm, c0, n in (("qsb", 0, 2), ("ksb", 512, 2), ("vsb", 1024, 2),
                          ("qfx", 1536, 2), ("kfx", 2048, 2), ("vfx", 2560, 2), ("gate", 3080, 8)):
            grp = "inA" if nm.endswith("sb") else ("inB" if nm != "gate" else "gate")
            pieces[nm] = [make_piece(f"{nm}{i}", grp, [(kview(w_in_d, c0 + 256 * i, c0 + 256 * (i + 1)), 8, 256)])
                          for i in range(n)]
        pieces["merge"] = [make_piece(f"mg{i}", "merge",
                                      [(w_bsb_d[:, 256 * i:256 * (i + 1)].rearrange("(k p) c -> p k c", p=128), 4, 256),
                                       (w_bfx_d[:, 256 * i:256 * (i + 1)].rearrange("(k p) c -> p k c", p=128), 4, 256)])
                           for i in range(4)]
        pieces["out"] = [make_piece(f"wo{i}", "out", [(kview(w_out_d, 256 * i, 256 * (i + 1)), 8, 256)])
                         for i in range(4)]
        pieces["gu"] = [make_piece(f"gu{j}", "gu",
                                   [(kview(w_fg_d, 128 * j, 128 * (j + 1)), 8, 128),
                                    (kview(w_fu_d, 128 * j, 128 * (j + 1)), 8, 128)])
                        for j in range(NJ)]
        JG = [(0, 4), (4, 8), (8, 12), (12, 16), (16, 20), (20, 22)]
        pieces["down"] = [[make_piece(f"dn{h}_{g}", "down",
                                      [(w_fd_d[j0 * 128:j1 * 128, 512 * h:512 * (h + 1)].rearrange("(j p) c -> p j c", p=128),
                                        j1 - j0, 512)])
                           for g, (j0, j1) in enumerate(JG)] for h in range(2)]

        flush_casts(["inA"])

        def small_load(t, tb, src):
            P.dma(sp, t, src, writes=[tb])

        small_load(cT[:], CT_B, cT_d)
        small_load(b_adaT[:], BADA_B, b_adaT_d)
        small_load(g_mixT[:], GMIX_B, g_mixT_d)
        small_load(g_ffnT[:], GFFN_B, g_ffnT_d)
        small_load(gfin[:], GFIN_B, gfin_d)
        small_load(bfor[:], BFOR_B, bfor_d)
        small_load(bgateT[:], BGATE_B, bgateT_d)
        P.dma(pool, wf[:], kview(w_in_d, 3072, 3080), writes=[WF_B])

        act_op(csig[:], cT[:], AF.Sigmoid, [CT_B], [CSIG_B])
        tt_op(dve, cact[:], cT[:], csig[:], ALU.mult, [CT_B, CSIG_B], [CACT_B])
        for pc in range(24):
            hb = pc % 2
            stg = X[:, 2 * hb:2 * hb + 2, :].rearrange("p a b -> p (a b)").rearrange("p (k c) -> p k c", k=8)
            stgb = [X_B[2 * hb], X_B[2 * hb + 1]]
            P.dma(sp, stg, w_ada_d[:, 256 * pc:256 * (pc + 1)].rearrange("(k p) c -> p k c", p=128),
                  writes=stgb, owner=stgb[0])
            bank, bankb = next_bank()
            for ct in range(2):
                for kc in range(8):
                    mm(bank[:, 2 * ct:2 * ct + 2], stg[:, kc, 128 * ct:128 * (ct + 1)], cact[:, kc, :],
                       kc == 0, kc == 7, stgb + [CACT_B], [bankb], skip=True)
            copy_op(dve, modraw[:, 2 * pc:2 * (pc + 1), :],
                    bank[:, 0:4].rearrange("p (a b) -> p a b", a=2), [bankb], [MODRAW_B])
        for b in range(NSEQ):
            tt_op(dve, modT[:, :, b], modraw[:, :, b], b_adaT[:], ALU.add, [MODRAW_B, BADA_B], [MODT_B])
        for b in range(NSEQ):
            for which, (v, gsrc, gbuf) in enumerate(((1, g_mixT, GMIX_B), (4, g_ffnT, GFFN_B))):
                P.op(dve, lambda e, b=b, which=which, v=v, gsrc=gsrc: e.scalar_tensor_tensor(
                    out=geff[:, b, which, :], in0=modT[:, 8 * v:8 * v + 8, b], scalar=1.0, in1=gsrc[:],
                    op0=ALU.add, op1=ALU.mult), reads=[MODT_B, gbuf], writes=[GEFF_B])

        def build_gate_bc(b):
            k = 0
            for gi, v in ((0, 2), (1, 5)):
                for half in range(2):
                    bank, bankb = next_bank()
                    for q in range(4):
                        ch = half * 4 + q
                        s = k % 2
                        k += 1
                        ts_op(dve, Gm[:, s, :], ones32[:], modT[:, 8 * v + ch, b:b + 1], None, ALU.mult, None,
                              [ON_B, MODT_B], [GM_B[s]])
                        mm(bank[:, 128 * q:128 * (q + 1)], Gm[:, s, :], ident[:], True, True,
                           [GM_B[s], ID_B], [bankb], skip=True)
                    copy_op(evac_engine(), GBC[gi][:, 512 * half:512 * (half + 1)], bank[:], [bankb], [GBC_B[gi][half]])

        junk = AR[:, 22:24, :]
        JUNK_B = [AR_B[22], AR_B[23]]

        def sumsq_rstd():
            for i in range(4):
                act_op(junk, X[:, i, :].rearrange("p (a b) -> p a b", a=2), AF.Square,
                       [X_B[i]], JUNK_B + [SS_B], accum_out=ss[:, i:i + 1])
            act_op(lnv[:], ss[:], AF.Ln, [SS_B], [LNV_B], scale=1.0 / D, bias=EPS)
            act_op(rstd[:], lnv[:], AF.Exp, [LNV_B], [RSTD_B], scale=-0.5)

        def norm_to_hT(b, which, vshift):
            sumsq_rstd()
            for i in range(4):
                ts_op(dve, Dm[:, i, :], ident[:], rstd[:, i:i + 1], None, ALU.mult, None, [ID_B, RSTD_B], [DM_B[i]])
            for kc in range(8):
                bank, bankb = next_bank()
                for i in range(4):
                    mm(bank[:, 128 * i:128 * (i + 1)], X[:, i, 128 * kc:128 * (kc + 1)], Dm[:, i, :], True, True,
                       [X_B[i], DM_B[i]], [bankb], skip=True)
                eng = evac_engine()
                sc = geff[:, b, which, kc:kc + 1]
                bi = modT[:, 8 * vshift + kc, b:b + 1]
                if eng is act:
                    act_op(HT[:, kc, :], bank[:], AF.Identity, [bankb, GEFF_B, MODT_B], [HT_B[kc]], scale=sc, bias=bi)
                else:
                    ts_op(dve, HT[:, kc, :], bank[:], sc, bi, ALU.mult, ALU.add, [bankb, GEFF_B, MODT_B], [HT_B[kc]])

        def load_x(b, c):
            for i in range(4):
                r0 = c * CH + i * 128
                P.dma(sp, X[:, i, :], x_d[b, r0:r0 + 128, :], writes=[X_B[i]])

        stream = {"n": 0}

        def run_steps(steps):
            pidx = [i for i, s_ in enumerate(steps) if s_[0] is not None]
            slot_of = {}
            ptr = 0
            for i, (piece, fn) in enumerate(steps):
                if piece is not None:
                    k = pidx.index(i)
                    while ptr < len(pidx) and ptr <= k + LOOKAHEAD:
                        t, tot, gb = steps[pidx[ptr]][0]
                        sl = stream["n"] % NSLOT
                        stream["n"] += 1
                        P.dma(sp, WS[sl][:, 0:tot], t[:, :], reads=[gb], writes=[WS_B[sl]])
                        slot_of[pidx[ptr]] = sl
                        ptr += 1
                    sl = slot_of[i]
                    fn(WS[sl], WS_B[sl])
                else:
                    fn(None, None)

        QT0 = 16

        def proj_fm(piece_w, piece_b, ncols_tiles, consume):
            wv = piece_w[:, 0:2048].rearrange("p (k c) -> p k c", k=8)
            for ct in range(ncols_tiles):
                bank, bankb = next_bank()
                for kc in range(8):
                    mm(bank[:], wv[:, kc, 128 * ct:128 * (ct + 1)], HT[:, kc, :], kc == 0, kc == 7,
                       [piece_b, HT_B[kc]], [bankb])
                consume(ct, bank, bankb)

        def q_steps(nm):
            steps = []
            for i in range(2):
                def fn(w, wb, i=i):
                    def consume(ct, bank, bankb):
                        pr = 2 * i + ct
                        eng = evac_engine()
                        if eng is act:
                            act_op(AR[:, QT0 + pr, :], bank[:], AF.Copy, [bankb], [AR_B[QT0 + pr]], scale=0.125)
                        else:
                            ts_op(dve, AR[:, QT0 + pr, :], bank[:], 0.125, None, ALU.mult, None, [bankb], [AR_B[QT0 + pr]])
                    proj_fm(w, wb, 2, consume)
                steps.append((pieces[nm][i], fn))
            return steps

        def k_steps(nm, c):
            steps = []
            for i in range(2):
                def fn(w, wb, i=i):
                    def consume(ct, bank, bankb):
                        pr = 2 * i + ct
                        copy_op(evac_engine(), KT[:, pr, c * CH:(c + 1) * CH], bank[:], [bankb], [KT_B[pr][c]])
                    proj_fm(w, wb, 2, consume)
                steps.append((pieces[nm][i], fn))
            return steps

        def v_steps(nm, c):
            steps = []
            for g in range(2):
                def fn(w, wb, g=g):
                    wv = w[:, 0:2048].rearrange("p (k c) -> p k c", k=8)
                    for i in range(4):
                        blk = 4 * c + i
                        bank, bankb = next_bank()
                        for kc in range(8):
                            mm(bank[:, 0:256], HT[:, kc, 128 * i:128 * (i + 1)], wv[:, kc, :], kc == 0, kc == 7,
                               [wb, HT_B[kc]], [bankb])
                        bv = bank[:, 0:256].rearrange("p (q t d) -> p q t d", q=2, t=2)
                        copy_op(act, Vv[:, blk, 2 * g:2 * g + 2, 0:64], bv[:, :, 0, :], [bankb], [V_B[blk]])
                        copy_op(dve, Vv[:, blk, 2 * g:2 * g + 2, 128:192], bv[:, :, 1, :], [bankb], [V_B[blk]])
                steps.append((pieces[nm][g], fn))
            return steps

        def make_bg(b, c, qt0):
            plist = pieces["qsb"] + pieces["ksb"] + pieces["vsb"]
            slots = {}
            st_ = {"ptr": 0}

            def ensure(n):
                while st_["ptr"] < len(plist) and st_["ptr"] <= n + LOOKAHEAD:
                    t, tot, gb = plist[st_["ptr"]]
                    sl = stream["n"] % NSLOT
                    stream["n"] += 1
                    P.dma(sp, WS[sl][:, 0:tot], t[:, :], reads=[gb], writes=[WS_B[sl]])
                    slots[st_["ptr"]] = sl
                    st_["ptr"] += 1
                return WS[slots[n]], WS_B[slots[n]]

            def pre():
                sumsq_rstd()
                for i in range(4):
                    ts_op(dve, Dm[:, i, :], ident[:], rstd[:, i:i + 1], None, ALU.mult, None, [ID_B, RSTD_B], [DM_B[i]])

            groups = []
            for kc in range(8):
                for hf in range(2):
                    def pe_part(BG, BGb, kc=kc, hf=hf):
                        for ii in range(2):
                            i = 2 * hf + ii
                            mm(BG[:, 128 * ii:128 * (ii + 1)], X[:, i, 128 * kc:128 * (kc + 1)], Dm[:, i, :], True, True,
                               [X_B[i], DM_B[i]], [BGb], skip=True)

                    def ev_part(BG, BGb, eng, kc=kc, hf=hf):
                        sc, bi = geff[:, b, 0, kc:kc + 1], modT[:, kc, b:b + 1]
                        o = HT[:, kc, 256 * hf:256 * (hf + 1)]
                        if eng is act:
                            act_op(o, BG[:, 0:256], AF.Identity, [BGb, GEFF_B, MODT_B], [HT_B[kc]], scale=sc, bias=bi)
                        else:
                            ts_op(dve, o, BG[:, 0:256], sc, bi, ALU.mult, ALU.add, [BGb, GEFF_B, MODT_B], [HT_B[kc]])
                    groups.append((pe_part, ev_part))
            for wi, nm in enumerate(("qsb", "ksb")):
                for i in range(2):
                    for ct in range(2):
                        for hf in range(2):
                            pr = 2 * i + ct

                            def pe_part(BG, BGb, n=2 * wi + i, ct=ct, hf=hf):
                                w, wb = ensure(n)
                                wv = w[:, 0:2048].rearrange("p (k c) -> p k c", k=8)
                                for kc in range(8):
                                    mm(BG[:, 0:256], wv[:, kc, 128 * ct:128 * (ct + 1)], HT[:, kc, 256 * hf:256 * (hf + 1)],
                                       kc == 0, kc == 7, [wb, HT_B[kc]], [BGb], skip=True)

                            if nm == "qsb":
                                def ev_part(BG, BGb, eng, pr=pr, hf=hf):
                                    o = AR[:, qt0 + pr, 256 * hf:256 * (hf + 1)]
                                    if eng is act:
                                        act_op(o, BG[:, 0:256], AF.Copy, [BGb], [AR_B[qt0 + pr]], scale=0.125)
                                    else:
                                        ts_op(dve, o, BG[:, 0:256], 0.125, None, ALU.mult, None, [BGb], [AR_B[qt0 + pr]])
                            else:
                                def ev_part(BG, BGb, eng, pr=pr, hf=hf):
                                    copy_op(eng, KT[:, pr, c * CH + 256 * hf:c * CH + 256 * (hf + 1)], BG[:, 0:256],
                                            [BGb], [KT_B[pr][c]])
                            groups.append((pe_part, ev_part))
            for g in range(2):
                for i in range(4):
                    def pe_part(BG, BGb, g=g, i=i):
                        w, wb = ensure(4 + g)
                        wv = w[:, 0:2048].rearrange("p (k c) -> p k c", k=8)
                        for kc in range(8):
                            mm(BG[:, 0:256], HT[:, kc, 128 * i:128 * (i + 1)], wv[:, kc, :], kc == 0, kc == 7,
                               [wb, HT_B[kc]], [BGb], skip=True)

                    def ev_part(BG, BGb, eng, g=g, i=i):
                        blk = 4 * c + i
                        bv = BG[:, 0:256].rearrange("p (q t d) -> p q t d", q=2, t=2)
                        copy_op(eng, Vv[:, blk, 2 * g:2 * g + 2, 0:64], bv[:, :, 0, :], [BGb], [V_B[blk]])
                        copy_op(dve, Vv[:, blk, 2 * g:2 * g + 2, 128:192], bv[:, :, 1, :], [BGb], [V_B[blk]])
                    groups.append((pe_part, ev_part))
            return dict(load=lambda: load_x(b, c), pre=pre, groups=groups)

        def v3(t, col0):
            return t[:, :].rearrange("p (s c) -> p s c", s=2)[:, :, col0:CH]

        def fill(bank_i, n, first=False):
            for j in range(n):
                P.op(pe, lambda e: e.matmul(PB[bank_i][:, 384:512], lhsT=triN[:], rhs=triN[:], start=True, stop=True,
                                            skip_group_check=True),
                     reads=[TRIN_B] if (first and j == 0) else [], writes=[PB_B[bank_i]] if (first and j == 0) else [])

        def sb_attention(c, qt0, bg):
            nst = 4 * c + 4
            nf = NF_SB if bg is None else 1
            Zp, Zb = PP[1], [PB_B[2], PB_B[3]]
            Cp, Cb = PP[2], [PB_B[4], PB_B[5]]
            XPp, XPb = GBC[1], GBC_B[1]
            for p in range(4):
                Y, Yb = PB[6], PB_B[6]
                qb = AR_B[qt0 + p]

                def kbof(k):
                    return nst - 1 - k

                def col0of(k):
                    kb = kbof(k)
                    return 128 * (kb - 4 * c) if kb >= 4 * c else 0

                def pe1(k):
                    kb, c0 = kbof(k), col0of(k)
                    for s in range(2):
                        mm(Zp[:, CH * s + c0:CH * (s + 1)], KT[64 * s:64 * s + 64, p, 128 * kb:128 * (kb + 1)],
                           AR[64 * s:64 * s + 64, qt0 + p, c0:CH], True, True,
                           [KT_B[p][kb // 4], qb], [Zb[s]])

                def act1(k):
                    c0 = col0of(k)
                    e, eb = EE3[k % 3], EE3_B[k % 3]
                    act_op(v3(e, c0), v3(Zp, c0), AF.Exp, Zb, eb)
                    if kbof(k) >= 4 * c:
                        n = CH - c0
                        P.op(pool, lambda en, e=e, c0=c0, n=n: en.affine_select(
                            out=v3(e, c0), in_=v3(e, c0), pattern=[[0, 2], [1, n]], compare_op=ALU.is_gt, fill=0.0,
                            base=0, channel_multiplier=-1), reads=eb, writes=eb)

                def act2(k):
                    c0 = col0of(k)
                    act_op(v3(SPP[k % 2], c0), v3(EE3[k % 3], c0), AF.Ln, EE3_B[k % 3], SPP_B[k % 2], bias=1.0)

                def pe2(k):
                    c0 = col0of(k)
                    for s in range(2):
                        mm(Cp[:, CH * s + c0:CH * (s + 1)], triN[:], SPP[k % 2][:, CH * s + c0:CH * (s + 1)], k == 0, True,
                           [TRIN_B, SPP_B[k % 2][s]], [Cb[s]], skip=True)

                def act3(k):
                    c0 = col0of(k)
                    act_op(v3(XPp, c0), v3(Cp, c0), AF.Exp, Cb, XPb)

                def pe3(k):
                    c0 = col0of(k)
                    for s in range(2):
                        mm(Cp[:, CH * s + c0:CH * (s + 1)], cmpN[:], SPP[k % 2][:, CH * s + c0:CH * (s + 1)], False, True,
                           [CMPN_B, SPP_B[k % 2][s]], [Cb[s]], skip=True)

                def dve_w(k):
                    c0 = col0of(k)
                    for s in range(2):
                        sl = slice(CH * s + c0, CH * (s + 1))
                        tt_op(dve, WW[:, sl], EE3[k % 3][:, sl], XPp[:, sl], ALU.mult,
                              [EE3_B[k % 3][s], XPb[s]], [WW_B[s]])

                def pe4(k):
                    kb, c0 = kbof(k), col0of(k)
                    for s in range(2):
                        vc = 192 * p + 128 * s
                        mm(Y[64 * s:64 * s + 64, c0:CH], V[:, kb, vc:vc + 64], WW[:, CH * s + c0:CH * (s + 1)],
                           k == 0, k == nst - 1, [V_B[kb], WW_B[s]], [Yb], skip=True)

                fill(7, 1, first=True)
                if bg is not None and p == 0:
                    bg["load"]()
                pe1(0)
                act1(0)
                pe1(1)
                act2(0)
                for k in range(nst):
                    if bg is not None and (bg["groups"] or not bg.get("drained")):
                        nf = 0
                        if not bg["groups"]:
                            bg["drained"] = True
                            fill(7, 1, first=True)
                    else:
                        nf = NF_SB
                    pe2(k)
                    if k > 0:
                        pe4(k - 1)
                    fill(7, nf)
                    if k + 1 < nst:
                        act1(k + 1)
                    if k + 2 < nst:
                        pe1(k + 2)
                        fill(7, nf)
                    act3(k)
                    if k < nst - 1:
                        pe3(k)
                        fill(7, nf)
                    grp = None
                    if bg is not None:
                        if p == 0 and k == 2:
                            bg["pre"]()
                        elif (p > 0 or k > 2) and bg["groups"]:
                            grp = bg["groups"].pop(0)
                            grp[0](PB[7], PB_B[7])
                    dve_w(k)
                    if grp is not None:
                        grp[1](PB[7], PB_B[7], dve)
                    if k + 1 < nst:
                        act2(k + 1)
                    if k == 4 and c >= 3:
                        emit_casts(5)
                pe4(nst - 1)
                copy_op(dve, YSB[:, p, c * CH:(c + 1) * CH], Y[:], [Yb], [YSB_B[p][c]])
            if bg is not None:
                while bg["groups"]:
                    grp = bg["groups"].pop(0)
                    bank, bankb = next_bank()
                    grp[0](bank, bankb)
                    grp[1](bank, bankb, evac_engine())

        def sq_of(hh):
            if hh % 2 == 0:
                return AR[:, 20 + hh // 2, :], AR_B[20 + hh // 2]
            return HT[:, 4 + hh // 2, :], HT_B[4 + hh // 2]

        def fox_prep_thunks(c):
            st8 = {}

            def prepA(i):
                def fn(w_, wb_):
                    bank, bankb = next_bank()
                    for kc in range(8):
                        mm(bank[:, 0:8], HT[:, kc, 128 * i:128 * (i + 1)], wf[:, kc, :], kc == 0, kc == 7,
                           [HT_B[kc], WF_B], [bankb])
                    tt_op(dve, ft[:], bank[:, 0:8], bfor[:], ALU.add, [bankb, BFOR_B], [FT_B])
                    act_op(fe[:], ft[:], AF.Exp, [FT_B], [FE_B], scale=-1.0)
                    act_op(fl[:, i, :], fe[:], AF.Ln, [FE_B], [FL_B[i]], bias=1.0)
                return fn

            def prepB(i):
                def fn(w_, wb_):
                    blk = 4 * c + i
                    bank2, bank2b = next_bank()
                    mm(bank2[:, 0:8], triIN[:], fl[:, i, :], True, blk == 0, [TRIIN_B, FL_B[i]], [bank2b], skip=True)
                    if blk > 0:
                        mm(bank2[:, 0:8], selL[:], ckT[:, blk - 1, :], False, True, [SELL_B, CK_B[blk - 1]], [bank2b], skip=True)
                    copy_op(dve, ckT[:, blk, :], bank2[:, 0:8], [bank2b], [CK_B[blk]])
                return fn

            def biasA(w_, wb_):
                bank3, bank3b = next_bank()
                mm(bank3[:, 0:8], selL[:], ckT[:, 4 * c + 1, :], True, True, [SELL_B, CK_B[4 * c + 1]], [bank3b], skip=True)
                copy_op(dve, crefbc[:], bank3[:, 0:8], [bank3b], [CREF_B])

            def biasB(w_, wb_):
                nkb = 4 * c + 4
                for hh in range(8):
                    ts_op(dve, biasT[:, hh, 0:nkb], ckT[:, 0:nkb, hh], crefbc[:, hh:hh + 1], -1.0, ALU.subtract, ALU.mult,
                          CK_B[0:nkb] + [CREF_B], [BIAS_B])

            def shiftq(w_, wb_):
                k = 0
                for hh in range(8):
                    bank4, bank4b = next_bank()
                    for i in range(4):
                        sl_ = k % 2
                        k += 1
                        ts_op(dve, Gm[:, sl_, :], ones32[:], ckT[:, 4 * c + i, hh:hh + 1], None, ALU.mult, None,
                              [ON_B, CK_B[4 * c + i]], [GM_B[sl_]])
                        mm(bank4[:, 128 * i:128 * (i + 1)], Gm[:, sl_, :], ident[:], True, True,
                           [GM_B[sl_], ID_B], [bank4b], skip=True)
                    dst, dstb = sq_of(hh)
                    ts_op(dve, dst, bank4[:], -1.0, crefbc[:, hh:hh + 1], ALU.mult, ALU.add,
                          [bank4b, CREF_B], [dstb])

            def seq(*fns):
                def fn(w_, wb_):
                    for f_ in fns:
                        f_(w_, wb_)
                return fn

            return [seq(prepA(0), prepA(1)), seq(prepA(2), prepA(3), prepB(0)), seq(prepB(1)), seq(prepB(2)),
                    seq(prepB(3)), seq(biasA), seq(biasB)], shiftq

        def fox_attention(c):
            nst = 4 * c + 4
            for p in range(4):
                ybanks = (6, 7) if p % 2 == 0 else (4, 5)
                heads = []
                for s in range(2):
                    r0 = 64 * s
                    heads.append(dict(
                        s=s, r0=r0, hh=2 * p + s,
                        qb=AR_B[QT0 + p],
                        vc=192 * p + 64 * s,
                        Z=[PB[2 + s], PB[s]], Zb=[PB_B[2 + s], PB_B[s]],
                        Y=PB[ybanks[s]], Yb=PB_B[ybanks[s]],
                        pb=[(SPP[0][:, CH * s:CH * (s + 1)], SPP_B[0][s]), (SPP[1][:, CH * s:CH * (s + 1)], SPP_B[1][s])]))

                def col0of(kb):
                    return 128 * (kb - 4 * c) if kb >= 4 * c else 0

                def pe1(h, kb):
                    c0 = col0of(kb)
                    r0 = h["r0"]
                    mm(h["Z"][kb % 2][:, c0:CH], KT[r0:r0 + 64, p, 128 * kb:128 * (kb + 1)], AR[r0:r0 + 64, QT0 + p, c0:CH],
                       True, True, [KT_B[p][kb // 4], h["qb"]], [h["Zb"][kb % 2]])

                def pe1s(h, kb):
                    c0 = col0of(kb)
                    sq, sqb = sq_of(h["hh"])
                    zt = h["Z"][kb % 2]
                    tt_op(dve, zt[:, c0:CH], zt[:, c0:CH], sq[:, c0:CH], ALU.subtract,
                          [h["Zb"][kb % 2], sqb], [h["Zb"][kb % 2]])

                for kb0 in range(2):
                    for h in heads:
                        pe1(h, kb0)
                    for h in heads:
                        pe1s(h, kb0)
                for kb in range(nst):
                    first, last = kb == 0, kb == nst - 1
                    c0 = col0of(kb)
                    n = CH - c0
                    for h in heads:
                        pt, ptb = h["pb"][kb % 2]
                        act_op(pt[:, c0:CH], h["Z"][kb % 2][:, c0:CH], AF.Exp, [h["Zb"][kb % 2], BIAS_B], [ptb],
                               bias=biasT[:, h["hh"], kb:kb + 1])
                        if kb >= 4 * c:
                            P.op(pool, lambda e, pt=pt, c0=c0, n=n: e.affine_select(
                                out=pt[:, c0:CH], in_=pt[:, c0:CH], pattern=[[1, n]], compare_op=ALU.is_ge, fill=0.0,
                                base=0, channel_multiplier=-1), reads=[ptb], writes=[ptb])
                    if kb + 2 < nst:
                        for h in heads:
                            pe1(h, kb + 2)
                        for h in heads:
                            pe1s(h, kb + 2)
                        fill(4, NF_FX)
                    for h in heads:
                        pt, ptb = h["pb"][kb % 2]
                        mm(h["Y"][:, c0:CH], V[:, kb, h["vc"]:h["vc"] + 128], pt[:, c0:CH], first, last,
                           [V_B[kb], ptb], [h["Yb"]], skip=True)
                    fill(4, NF_FX)
                for h in heads:
                    s = h["s"]
                    yr = slice(0, 64) if s == 0 else slice(64, 128)
                    dr = slice(64, 128) if s == 0 else slice(0, 64)
                    P.op(dve, lambda e, h=h, s=s, dr=dr: e.reciprocal(out=XC[s][dr, :], in_=h["Y"][dr, :]),
                         reads=[h["Yb"]], writes=[XC_B[s]])
                    tt_op(dve, HT[yr, p, :], h["Y"][yr, :], XC[s][dr, :], ALU.mult, [h["Yb"], XC_B[s]], [HT_B[p]])

        def gate_steps():
            steps = []
            for i in range(8):
                def fn(w, wb, i=i):
                    def consume(ct, bank, bankb):
                        j = 2 * i + ct
                        act_op(AR[:, j, :], bank[:], AF.Sigmoid, [bankb, BGATE_B], [AR_B[j]], bias=bgateT[:, j:j + 1])
                    proj_fm(w, wb, 2, consume)
                steps.append((pieces["gate"][i], fn))
            return steps

        def merge_steps(c):
            steps = []
            for i in range(4):
                def fn(w, wb, i=i):
                    wv = w[:, 0:2048].rearrange("p (k c) -> p k c", k=8)
                    for ct in range(2):
                        j = 2 * i + ct
                        b1, b1b = next_bank()
                        for kc in range(4):
                            mm(b1[:], wv[:, kc, 128 * ct:128 * (ct + 1)], YSB[:, kc, c * CH:(c + 1) * CH], kc == 0, kc == 3,
                               [wb, YSB_B[kc][c]], [b1b])
                        b2, b2b = next_bank()
                        for kc in range(4):
                            mm(b2[:], wv[:, 4 + kc, 128 * ct:128 * (ct + 1)], HT[:, kc, :], kc == 0, kc == 3,
                               [wb, HT_B[kc]], [b2b])
                        t1, t1b = (E_[ct], E_B[ct])
                        t2, t2b = (XC[ct], XC_B[ct])
                        tt_op(dve, t1[:], AR[:, j, :], b1[:], ALU.mult, [AR_B[j], b1b], [t1b])
                        tt_op(dve, t2[:], AR[:, 8 + j, :], b2[:], ALU.mult, [AR_B[8 + j], b2b], [t2b])
                        tt_op(pool if ct == 0 else dve, AR[:, 16 + j, :], t1[:], t2[:], ALU.add, [t1b, t2b], [AR_B[16 + j]])
                steps.append((pieces["merge"][i], fn))
            return steps

        ALT_T = [(SPP[0][:, :].bitcast(F32), SPP_B[0]), (SPP[1][:, :].bitcast(F32), SPP_B[1]),
                 (WW[:, :].bitcast(F32), WW_B)]

        def resid_update(i, cols, bank, bankb, gi, half_b, k, alt=False):
            if alt:
                t, tbs = ALT_T[k % 3]
            else:
                t, tb = (E_[k % 2], E_B[k % 2]) if (k // 2) % 2 == 0 else (XC[k % 2], XC_B[k % 2])
                tbs = [tb]
            n = cols.stop - cols.start
            tt_op(dve, t[:, 0:n], bank[:, 0:n], GBC[gi][:, cols], ALU.mult, [bankb, half_b], tbs)
            tt_op(pool if k % 2 == 0 else dve, X[:, i, cols], X[:, i, cols], t[:, 0:n], ALU.add, [X_B[i]] + tbs, [X_B[i]])

        def out_steps():
            steps = []
            for q in range(4):
                def fn(w, wb, q=q):
                    wv = w[:, 0:2048].rearrange("p (k c) -> p k c", k=8)
                    for i in range(4):
                        bank, bankb = next_bank()
                        for kc in range(8):
                            mm(bank[:, 0:256], AR[:, 16 + kc, 128 * i:128 * (i + 1)], wv[:, kc, :], kc == 0, kc == 7,
                               [wb, AR_B[16 + kc]], [bankb])
                        resid_update(i, slice(256 * q, 256 * (q + 1)), bank, bankb, 0, GBC_B[0][q // 2], i)
                steps.append((pieces["out"][q], fn))
            return steps

        def gu_steps():
            steps = []
            for j in range(NJ):
                def fn(w, wb, j=j):
                    wv = w[:, 0:2048].rearrange("p (t k c) -> p t k c", t=2, k=8)
                    bg, bgb = next_bank()
                    for kc in range(8):
                        mm(bg[:], wv[:, 0, kc, :], HT[:, kc, :], kc == 0, kc == 7, [wb, HT_B[kc]], [bgb])
                    bu, bub = next_bank()
                    for kc in range(8):
                        mm(bu[:], wv[:, 1, kc, :], HT[:, kc, :], kc == 0, kc == 7, [wb, HT_B[kc]], [bub])
                    t, tb = (E_[j % 2], E_B[j % 2])
                    act_op(t[:], bg[:], AF.Silu, [bgb], [tb])
                    tt_op(dve, AR[:, j, :], t[:], bu[:], ALU.mult, [tb, bub], [AR_B[j]])
                steps.append((pieces["gu"][j], fn))
            return steps

        def make_pnorm(b, cn):
            def load(wv):
                def fn():
                    for ii in range(2):
                        r0 = cn * CH + (2 * wv + ii) * 128
                        P.dma(sp, EE[ii][:, :], x_d[b, r0:r0 + 128, :], writes=EE_B[ii], owner=EE_B[ii][0])
                return fn

            def partA(wv):
                def fn():
                    for ii in range(2):
                        i = 2 * wv + ii
                        act_op(junk, EE[ii][:, :].rearrange("p (a b) -> p a b", a=2), AF.Square,
                               EE_B[ii], JUNK_B + [SS_B], accum_out=ss[:, i:i + 1])
                    act_op(lnv[:, 2 * wv:2 * wv + 2], ss[:, 2 * wv:2 * wv + 2], AF.Ln, [SS_B], [LNV_B], scale=1.0 / D, bias=EPS)
                    act_op(rstd[:, 2 * wv:2 * wv + 2], lnv[:, 2 * wv:2 * wv + 2], AF.Exp, [LNV_B], [RSTD_B], scale=-0.5)
                    for ii in range(2):
                        i = 2 * wv + ii
                        ts_op(dve, Dm[:, i, :], ident[:], rstd[:, i:i + 1], None, ALU.mult, None, [ID_B, RSTD_B], [DM_B[i]])
                return fn

            def partB(wv):
                def fn():
                    for kc in range(8):
                        bank, bankb = next_bank()
                        for ii in range(2):
                            i = 2 * wv + ii
                            mm(bank[:, 128 * ii:128 * (ii + 1)], EE[ii][:, 128 * kc:128 * (kc + 1)], Dm[:, i, :], True, True,
                               EE_B[ii] + [DM_B[i]], [bankb], skip=True)
                        eng = evac_engine()
                        sc = geff[:, b, 0, kc:kc + 1]
                        bi = modT[:, kc, b:b + 1]
                        o = HT[:, kc, 256 * wv:256 * (wv + 1)]
                        if eng is act:
                            act_op(o, bank[:, 0:256], AF.Identity, [bankb, GEFF_B, MODT_B], [HT_B[kc]], scale=sc, bias=bi)
                        else:
                            ts_op(dve, o, bank[:, 0:256], sc, bi, ALU.mult, ALU.add, [bankb, GEFF_B, MODT_B], [HT_B[kc]])
                return fn

            return {(0, 0): [load(0)], (0, 1): [partA(0)], (0, 3): [partB(0), load(1)], (0, 5): [partA(1)],
                    (1, 1): [partB(1)]}

        def down_steps(hooks):
            steps = []
            ACC = [(PB[4], PB_B[4]), (PB[5], PB_B[5]), (PB[6], PB_B[6]), (PB[7], PB_B[7])]
            for hf in range(2):
                for g, (j0, j1) in enumerate(JG):
                    def fn(w, wb, hf=hf, g=g, j0=j0, j1=j1):
                        for hk in hooks.get((hf, g), []):
                            hk()
                        wv = w[:, 0:(j1 - j0) * 512].rearrange("p (j c) -> p j c", j=j1 - j0)
                        for i in range(4):
                            for j in range(j0, j1):
                                mm(ACC[i][0][:], AR[:, j, 128 * i:128 * (i + 1)], wv[:, j - j0, :], j == 0, j == NJ - 1,
                                   [wb, AR_B[j]], [ACC[i][1]])
                        if j1 == NJ:
                            for i in range(4):
                                resid_update(i, slice(512 * hf, 512 * (hf + 1)), ACC[i][0], ACC[i][1], 1, GBC_B[1][hf], i,
                                             alt=True)
                    steps.append((pieces["down"][hf][g], fn))
            return steps

        def final_norm_store(b, c, prefetch_next):
            sumsq_rstd()
            for i in range(4):
                if i < 2:
                    stg, stgb = EE[i][:, :], EE_B[i]
                else:
                    stg, stgb = X[:, i, :], [X_B[i]]
                P.op(dve, lambda e, i=i, stg=stg: e.scalar_tensor_tensor(
                    out=stg, in0=X[:, i, :], scalar=rstd[:, i:i + 1], in1=gfin[:],
                    op0=ALU.mult, op1=ALU.mult), reads=[X_B[i], RSTD_B, GFIN_B], writes=stgb)
                r0 = c * CH + i * 128
                P.dma(pool, out_d[b, r0:r0 + 128, :], stg, reads=stgb, owner=stgb[0], is_store=True)
                if prefetch_next and i < 2:
                    r1 = (c + 1) * CH + i * 128
                    P.dma(pool, X[:, i, :], x_d[b, r1:r1 + 128, :], writes=[X_B[i]])
            if prefetch_next:
                for i in (2, 3):
                    r1 = (c + 1) * CH + i * 128
                    P.dma(pool, X[:, i, :], x_d[b, r1:r1 + 128, :], writes=[X_B[i]])

        for b in range(NSEQ):
            steps = [(None, lambda w, wb: (load_x(b, 0), norm_to_hT(b, 0, 0)))]
            steps += q_steps("qsb") + k_steps("ksb", 0) + v_steps("vsb", 0)
            run_steps(steps)
            for c in range(NCH):
                qt0 = QT0 if c % 2 == 0 else 0
                qt0n = QT0 if (c + 1) % 2 == 0 else 0
                bg = make_bg(b, c + 1, qt0n) if c + 1 < NCH else None
                sb_attention(c, qt0, bg)
            emit_casts(10000)
            build_gate_bc(b)
            all_steps = []
            for c in range(NCH):
                if c == 0:
                    steps = [(None, lambda w, wb, c=c: (load_x(b, c), norm_to_hT(b, 0, 0)))]
                else:
                    steps = []
                steps += q_steps("qfx") + k_steps("kfx", c) + v_steps("vfx", c)
                th, shq = fox_prep_thunks(c)
                gs = gate_steps()
                for gi in range(8):
                    if gi < len(th):
                        steps.append((None, th[gi]))
                    steps.append(gs[gi])
                steps.append((None, shq))
                steps += [(None, lambda w, wb, c=c: fox_attention(c))]
                steps += merge_steps(c) + out_steps()
                steps += [(None, lambda w, wb: norm_to_hT(b, 1, 3))]
                steps += gu_steps() + down_steps(make_pnorm(b, c + 1) if c + 1 < NCH else {})
                steps += [(None, lambda w, wb, c=c: final_norm_store(b, c, c + 1 < NCH))]
                all_steps += steps
            run_steps(all_steps)

        P.wait_all(sp, X_B + EE_B[0] + EE_B[1])
        P.emit()
    return nc


_NC_CACHE = {}


def kernel(x, c, w_ada, b_ada, g_mix, w_in, b_forget, b_gate, w_branch_sb, w_branch_fox,
           w_out, g_ffn, w_ffn_gate, w_ffn_up, w_ffn_down, g_final):
    f = lambda a: np.ascontiguousarray(np.asarray(a, dtype=np.float32))
    x = f(x)
    c = f(c)
    if "nc" not in _NC_CACHE:
        _NC_CACHE["nc"] = build_nc()
    nc = _NC_CACHE["nc"]

    def featT(v, nch):
        return f(np.asarray(v, dtype=np.float32).reshape(nch, 128).T)

    shared = {
        "w_ada": f(w_ada[0]),
        "b_adaT": featT(b_ada[0], 48),
        "g_mixT": featT(g_mix[0], 8),
        "g_ffnT": featT(g_ffn[0], 8),
        "gfin_bc": f(np.broadcast_to(np.asarray(g_final, dtype=np.float32)[None, :], (128, D))),
        "w_in": f(w_in[0]),
        "bfor_bc": f(np.broadcast_to(np.asarray(b_forget[0], dtype=np.float32)[None, :], (128, 8))),
        "b_gateT": featT(b_gate[0], 16),
        "w_bsb": f(w_branch_sb[0]),
        "w_bfx": f(w_branch_fox[0]),
        "w_out": f(w_out[0]),
        "w_fg": f(w_ffn_gate[0]),
        "w_fu": f(w_ffn_up[0]),
        "w_fd": f(w_ffn_down[0]),
    }
    in_maps = []
    for i in range(NCORES):
        m = dict(shared)
        m["x"] = x[NSEQ * i:NSEQ * (i + 1)]
        cc = c[NSEQ * i:NSEQ * (i + 1)]
        m["cT"] = f(cc.reshape(NSEQ, 8, 128).transpose(2, 1, 0))
        in_maps.append(m)
    res = run_bass_kernel_spmd(nc, in_maps, core_ids=list(range(NCORES)))
    out = np.concatenate([np.asarray(r["out"]) for r in res.results], axis=0)
    return out.astype(np.float32, copy=False)
```

```python
import numpy as np
from contextlib import ExitStack
import concourse.bass as bass
import concourse.mybir as mybir
from concourse.bass_utils import run_bass_kernel_spmd

F32 = mybir.dt.float32
BF16 = mybir.dt.bfloat16
AF = mybir.ActivationFunctionType
ALU = mybir.AluOpType

SEM_ROLL = 30000
SAME_ENGINE_SYNC = True

D = 1024
S = 4096
NSEQ = 2
CH = 512
NCH = S // CH
DFF = 2816
NJ = DFF // 128
EPS = 1e-6
NCORES = 8
NSLOT = 3
SLOT_ELEMS = 2048
LOOKAHEAD = NSLOT - 1
NF_SB = 4
NF_FX = 0


class Buf:
    __slots__ = ("name", "w", "r", "dsem_in", "dcnt_in", "dsem_out", "dcnt_out")

    def __init__(self, name):
        self.name = name
        self.w = {}
        self.r = {}
        self.dsem_in = None
        self.dcnt_in = 0
        self.dsem_out = None
        self.dcnt_out = 0


class Eng:
    def __init__(self, prog, name, is_pe=False):
        self.prog = prog
        self.name = name
        self.is_pe = is_pe
        self.ops = []
        self.count = 0
        self.sem = prog.new_sem(name)
        self.waited = {}

    def roll(self):
        if self.count >= SEM_ROLL:
            self.sem = self.prog.new_sem(self.name)
            self.count = 0


class Prog:
    def __init__(self, nc, stack):
        self.nc = nc
        self.stack = stack
        self.nsem = 0
        self.pe = Eng(self, "pe", True)
        self.act = Eng(self, "act")
        self.dve = Eng(self, "dve")
        self.pool = Eng(self, "pool")
        self.sp = Eng(self, "sp")

    def new_sem(self, name):
        self.nsem += 1
        return self.stack.enter_context(self.nc.semaphore(f"s{self.nsem}_{name}"))

    def _deps(self, eng, reads, writes):
        deps = {}
        for b in reads:
            for s, v in b.w.items():
                if deps.get(s, 0) < v:
                    deps[s] = v
        for b in writes:
            for d in (b.w, b.r):
                for s, v in d.items():
                    if deps.get(s, 0) < v:
                        deps[s] = v
        waits = []
        for s, v in deps.items():
            if s is eng.sem and (eng.is_pe or not SAME_ENGINE_SYNC):
                continue
            if eng.waited.get(s, 0) < v:
                eng.waited[s] = v
                waits.append((s, v))
        return waits

    def op(self, eng, fn, reads=(), writes=()):
        eng.roll()
        waits = self._deps(eng, reads, writes)
        eng.count += 1
        tok = (eng.sem, eng.count)
        eng.ops.append((waits, fn, tok[0], 1))
        for b in reads:
            if b.r.get(tok[0], 0) < tok[1]:
                b.r[tok[0]] = tok[1]
        for b in writes:
            b.w = {tok[0]: tok[1]}
            b.r = {}
        return tok

    def dma(self, eng, out_ap, in_ap, reads=(), writes=(), owner=None, is_store=False):
        waits = self._deps(eng, reads, writes)
        if owner is None:
            owner = writes[0] if writes else reads[0]
        if is_store:
            if owner.dsem_out is None:
                owner.dsem_out = self.new_sem("do_" + owner.name)
            owner.dcnt_out += 16
            tok = (owner.dsem_out, owner.dcnt_out)
        else:
            if owner.dsem_in is None:
                owner.dsem_in = self.new_sem("di_" + owner.name)
            owner.dcnt_in += 16
            tok = (owner.dsem_in, owner.dcnt_in)
        fn = (lambda e, o=out_ap, i=in_ap: e.dma_start(out=o, in_=i))
        eng.ops.append((waits, fn, tok[0], 16))
        for b in reads:
            if b.r.get(tok[0], 0) < tok[1]:
                b.r[tok[0]] = tok[1]
        for b in writes:
            b.w = {tok[0]: tok[1]}
            b.r = {}
        return tok

    def wait_all(self, eng, bufs):
        waits = self._deps(eng, (), bufs)
        eng.ops.append((waits, None, None, 0))

    def emit(self):
        nc = self.nc
        with nc.Block() as block:
            def run(e, eng):
                for waits, fn, sem, inc in eng.ops:
                    for s, v in waits:
                        e.wait_ge(s, v)
                    if fn is not None:
                        fn(e).then_inc(sem, inc)

            @block.tensor
            def _(e):
                run(e, self.pe)

            @block.scalar
            def _(e):
                run(e, self.act)

            @block.vector
            def _(e):
                run(e, self.dve)

            @block.gpsimd
            def _(e):
                run(e, self.pool)

            @block.sync
            def _(e):
                run(e, self.sp)


def build_nc():
    nc = bass.Bass("TRN2", target_bir_lowering=False)

    def din(name, shape):
        return nc.dram_tensor(name, shape, F32, kind="ExternalInput").ap()

    x_d = din("x", [NSEQ, S, D])
    cT_d = din("cT", [128, 8, NSEQ])
    w_ada_d = din("w_ada", [D, 6 * D])
    b_adaT_d = din("b_adaT", [128, 48])
    g_mixT_d = din("g_mixT", [128, 8])
    g_ffnT_d = din("g_ffnT", [128, 8])
    gfin_d = din("gfin_bc", [128, D])
    w_in_d = din("w_in", [D, 5128])
    bfor_d = din("bfor_bc", [128, 8])
    bgateT_d = din("b_gateT", [128, 16])
    w_bsb_d = din("w_bsb", [512, D])
    w_bfx_d = din("w_bfx", [512, D])
    w_out_d = din("w_out", [D, D])
    w_fg_d = din("w_fg", [D, DFF])
    w_fu_d = din("w_fu", [D, DFF])
    w_fd_d = din("w_fd", [DFF, D])
    out_d = nc.dram_tensor("out", [NSEQ, S, D], F32, kind="ExternalOutput").ap()

    with ExitStack() as st:
        P = Prog(nc, st)
        pe, act, dve, pool, sp = P.pe, P.act, P.dve, P.pool, P.sp

        def sb(name, shape, dt):
            return st.enter_context(nc.sbuf_tensor("sb_" + name, shape, dt))

        def ps(name):
            return st.enter_context(nc.psum_tensor(name, [128, 512], F32))

        KT = sb("KT", [128, 4, S], BF16)
        KT_B = [[Buf(f"KT{p}_{c}") for c in range(NCH)] for p in range(4)]
        V = sb("V", [128, 32, 768], BF16)
        V_B = [Buf(f"V{b}") for b in range(32)]
        YSB = sb("YSB", [128, 4, S], BF16)
        YSB_B = [[Buf(f"YSB{p}_{c}") for c in range(NCH)] for p in range(4)]
        X = sb("X", [128, 4, D], F32)
        X_B = [Buf(f"X{i}") for i in range(4)]
        HT = sb("HT", [128, 8, CH], BF16)
        HT_B = [Buf(f"HT{k}") for k in range(8)]
        AR = sb("AR", [128, 24, CH], BF16)
        AR_B = [Buf(f"AR{k}") for k in range(24)]
        WS = [sb(f"WS{i}", [128, SLOT_ELEMS], BF16) for i in range(NSLOT)]
        WS_B = [Buf(f"WS{i}") for i in range(NSLOT)]
        EE = [sb(f"ee{k}", [128, 2 * CH], F32) for k in range(2)]
        EE_B = [[Buf(f"ee{k}_{s}") for s in range(2)] for k in range(2)]
        E_ = [EE[0][:, CH * s:CH * (s + 1)] for s in range(2)]
        E_B = EE_B[0]
        XC = [EE[1][:, CH * s:CH * (s + 1)] for s in range(2)]
        XC_B = EE_B[1]
        SPP = [sb(f"spp{k}", [128, 2 * CH], BF16) for k in range(2)]
        SPP_B = [[Buf(f"spp{k}_{s}") for s in range(2)] for k in range(2)]
        WW = sb("ww", [128, 2 * CH], BF16)
        WW_B = [Buf(f"ww{s}") for s in range(2)]
        GBC = [sb(f"gbc{g}", [128, D], F32) for g in range(2)]
        GBC_B = [[Buf(f"gbc{g}_{h}") for h in range(2)] for g in range(2)]
        EE3 = [EE[0], EE[1], GBC[0]]
        EE3_B = [EE_B[0], EE_B[1], GBC_B[0]]
        gfin = sb("gfin", [128, D], F32)
        GFIN_B = Buf("gfin")
        ident = sb("ident", [128, 128], F32)
        ID_B = Buf("ident")
        ones32 = sb("ones32", [128, 128], F32)
        ON_B = Buf("ones32")
        triIN = sb("triIN", [128, 128], F32)
        TRIIN_B = Buf("triIN")
        selL = sb("selL", [128, 128], F32)
        SELL_B = Buf("selL")
        triN = sb("triN", [128, 128], BF16)
        TRIN_B = Buf("triN")
        cmpN = sb("cmpN", [128, 128], BF16)
        CMPN_B = Buf("cmpN")
        tmpb = sb("tmpb", [128, 128], BF16)
        TMPB_B = Buf("tmpb")
        Dm = sb("Dm", [128, 4, 128], F32)
        DM_B = [Buf(f"Dm{i}") for i in range(4)]
        Gm = sb("Gm", [128, 2, 128], F32)
        GM_B = [Buf(f"Gm{i}") for i in range(2)]
        ckT = sb("ckT", [128, 32, 8], F32)
        CK_B = [Buf(f"ck{b}") for b in range(32)]
        biasT = sb("biasT", [128, 8, 32], F32)
        BIAS_B = Buf("biasT")
        crefbc = sb("crefbc", [128, 8], F32)
        CREF_B = Buf("cref")
        ft = sb("ft", [128, 8], F32)
        FT_B = Buf("ft")
        fe = sb("fe", [128, 8], F32)
        FE_B = Buf("fe")
        fl = sb("fl", [128, 4, 8], F32)
        FL_B = [Buf(f"fl{i}") for i in range(4)]
        ss = sb("ss", [128, 4], F32)
        SS_B = Buf("ss")
        lnv = sb("lnv", [128, 4], F32)
        LNV_B = Buf("lnv")
        rstd = sb("rstd", [128, 4], F32)
        RSTD_B = Buf("rstd")
        cT = sb("cT", [128, 8, NSEQ], F32)
        CT_B = Buf("cT")
        cact = sb("cact", [128, 8, NSEQ], F32)
        CACT_B = Buf("cact")
        csig = sb("csig", [128, 8, NSEQ], F32)
        CSIG_B = Buf("csig")
        modraw = sb("modraw", [128, 48, NSEQ], F32)
        MODRAW_B = Buf("modraw")
        modT = sb("modT", [128, 48, NSEQ], F32)
        MODT_B = Buf("modT")
        b_adaT = sb("b_adaT", [128, 48], F32)
        BADA_B = Buf("b_adaT")
        g_mixT = sb("g_mixT", [128, 8], F32)
        GMIX_B = Buf("g_mixT")
        g_ffnT = sb("g_ffnT", [128, 8], F32)
        GFFN_B = Buf("g_ffnT")
        geff = sb("geff", [128, NSEQ, 2, 8], F32)
        GEFF_B = Buf("geff")
        bfor = sb("bfor", [128, 8], F32)
        BFOR_B = Buf("bfor")
        bgateT = sb("bgateT", [128, 16], F32)
        BGATE_B = Buf("bgateT")
        wf = sb("wf", [128, 8, 8], BF16)
        WF_B = Buf("wf")

        PP = [st.enter_context(nc.psum_tensor(f"pp{j}", [128, 2 * CH], F32)) for j in range(4)]
        PB = [PP[i // 2][:, CH * (i % 2):CH * (i % 2 + 1)] for i in range(8)]
        PB_B = [Buf(f"pb{i}") for i in range(8)]
        rot = {"i": 0}

        def next_bank():
            k = (0, 1, 2, 3)[rot["i"] % 4]
            rot["i"] += 1
            return PB[k], PB_B[k]

        def mm(out_ap, lhsT, rhs, start, stop, reads, writes, skip=False):
            if skip:
                P.op(pe, lambda e: e.matmul(out_ap, lhsT=lhsT, rhs=rhs, start=start, stop=stop,
                                            skip_group_check=True), reads=reads, writes=writes)
            else:
                P.op(pe, lambda e: e.matmul(out_ap, lhsT=lhsT, rhs=rhs, start=start, stop=stop),
                     reads=reads, writes=writes)

        def act_op(out, in_, func, reads, writes, **kw):
            P.op(act, lambda e: e.activation(out=out, in_=in_, func=func, **kw), reads=reads, writes=writes)

        def ts_op(eng, out, in0, s1, s2, op0, op1, reads, writes):
            if s2 is None:
                P.op(eng, lambda e: e.tensor_scalar(out=out, in0=in0, scalar1=s1, scalar2=None, op0=op0),
                     reads=reads, writes=writes)
            else:
                P.op(eng, lambda e: e.tensor_scalar(out=out, in0=in0, scalar1=s1, scalar2=s2, op0=op0, op1=op1),
                     reads=reads, writes=writes)

        def tt_op(eng, out, in0, in1, op, reads, writes):
            P.op(eng, lambda e: e.tensor_tensor(out=out, in0=in0, in1=in1, op=op), reads=reads, writes=writes)

        def copy_op(eng, out, in_, reads, writes):
            if eng is act:
                act_op(out, in_, AF.Copy, reads, writes)
            else:
                P.op(eng, lambda e: e.tensor_copy(out=out, in_=in_), reads=reads, writes=writes)

        evc = {"i": 0}

        def evac_engine():
            evc["i"] += 1
            return act if evc["i"] % 2 == 0 else dve

        scr_groups = {}
        cast_q = {}

        def make_piece(name, group, srcs):
            tot = sum(a * b for _, a, b in srcs)
            assert tot <= SLOT_ELEMS, (name, tot)
            t = nc.dram_tensor("scr_" + name, [128, tot], BF16, kind="Internal").ap()
            if group not in scr_groups:
                scr_groups[group] = Buf("scr_" + group)
            gb = scr_groups[group]
            off = 0
            for src, a, b in srcs:
                cast_q.setdefault(group, []).append(
                    (t[:, off:off + a * b].rearrange("p (a b) -> p a b", a=a), src, gb))
                off += a * b
            return (t, tot, gb)

        def flush_casts(groups):
            for g in groups:
                for dst, src, gb in cast_q.pop(g, []):
                    P.dma(pool, dst, src, writes=[gb], owner=gb)

        def emit_casts(n):
            for g in ("inB", "gate", "merge", "out", "gu", "down"):
                while n > 0 and cast_q.get(g):
                    dst, src, gb = cast_q[g].pop(0)
                    P.dma(pool, dst, src, writes=[gb], owner=gb)
                    n -= 1

        def kview(w, c0, c1):
            return w[:, c0:c1].rearrange("(k p) c -> p k c", p=128)

        P.op(pool, lambda e: e.memset(ones32[:], 1.0), writes=[ON_B])
        P.op(pool, lambda e: e.affine_select(out=ident[:], in_=ones32[:], pattern=[[-1, 128]],
                                             compare_op=ALU.is_equal, fill=0.0, base=0, channel_multiplier=1),
             reads=[ON_B], writes=[ID_B])
        ts_op(dve, selL[:], ones32[:], -1.0, None, ALU.mult, None, [ON_B], [SELL_B])
        P.op(pool, lambda e: e.affine_select(out=triIN[:], in_=selL[:], pattern=[[1, 128]],
                                             compare_op=ALU.is_ge, fill=0.0, base=0, channel_multiplier=-1),
             reads=[SELL_B], writes=[TRIIN_B])
        copy_op(dve, tmpb[:], selL[:], [SELL_B], [TMPB_B])
        P.op(pool, lambda e: e.affine_select(out=triN[:], in_=tmpb[:], pattern=[[-1, 128]],
                                             compare_op=ALU.is_ge, fill=0.0, base=0, channel_multiplier=1),
             reads=[TMPB_B], writes=[TRIN_B])
        tt_op(dve, cmpN[:], tmpb[:], triN[:], ALU.subtract, [TMPB_B, TRIN_B], [CMPN_B])
        P.op(pool, lambda e: e.affine_select(out=selL[:], in_=ones32[:], pattern=[[0, 128]],
                                             compare_op=ALU.is_ge, fill=0.0, base=-127, channel_multiplier=1),
             reads=[ON_B, TRIIN_B, TMPB_B], writes=[SELL_B])
        Vv = V[:].rearrange("p k (q c) -> p k q c", q=4)
        P.op(pool, lambda e: e.memset(Vv[:, :, :, 64:128], 1.0), writes=V_B)

        pieces = {}
        for nm, c0, n in (("qsb", 0, 2), ("ksb", 512, 2), ("vsb", 1024, 2),
                          ("qfx", 1536, 2), ("kfx", 2048, 2), ("vfx", 2560, 2), ("gate", 3080, 8)):
            grp = "inA" if nm.endswith("sb") else ("inB" if nm != "gate" else "gate")
            pieces[nm] = [make_piece(f"{nm}{i}", grp, [(kview(w_in_d, c0 + 256 * i, c0 + 256 * (i + 1)), 8, 256)])
                          for i in range(n)]
        pieces["merge"] = [make_piece(f"mg{i}", "merge",
                                      [(w_bsb_d[:, 256 * i:256 * (i + 1)].rearrange("(k p) c -> p k c", p=128), 4, 256),
                                       (w_bfx_d[:, 256 * i:256 * (i + 1)].rearrange("(k p) c -> p k c", p=128), 4, 256)])
                           for i in range(4)]
        pieces["out"] = [make_piece(f"wo{i}", "out", [(kview(w_out_d, 256 * i, 256 * (i + 1)), 8, 256)])
                         for i in range(4)]
        pieces["gu"] = [make_piece(f"gu{j}", "gu",
                                   [(kview(w_fg_d, 128 * j, 128 * (j + 1)), 8, 128),
                                    (kview(w_fu_d, 128 * j, 128 * (j + 1)), 8, 128)])
                        for j in range(NJ)]
        JG = [(0, 4), (4, 8), (8, 12), (12, 16), (16, 20), (20, 22)]
        pieces["down"] = [[make_piece(f"dn{h}_{g}", "down",
                                      [(w_fd_d[j0 * 128:j1 * 128, 512 * h:512 * (h + 1)].rearrange("(j p) c -> p j c", p=128),
                                        j1 - j0, 512)])
                           for g, (j0, j1) in enumerate(JG)] for h in range(2)]

        flush_casts(["inA"])

        def small_load(t, tb, src):
            P.dma(sp, t, src, writes=[tb])

        small_load(cT[:], CT_B, cT_d)
        small_load(b_adaT[:], BADA_B, b_adaT_d)
        small_load(g_mixT[:], GMIX_B, g_mixT_d)
        small_load(g_ffnT[:], GFFN_B, g_ffnT_d)
        small_load(gfin[:], GFIN_B, gfin_d)
        small_load(bfor[:], BFOR_B, bfor_d)
        small_load(bgateT[:], BGATE_B, bgateT_d)
        P.dma(pool, wf[:], kview(w_in_d, 3072, 3080), writes=[WF_B])

        act_op(csig[:], cT[:], AF.Sigmoid, [CT_B], [CSIG_B])
        tt_op(dve, cact[:], cT[:], csig[:], ALU.mult, [CT_B, CSIG_B], [CACT_B])
        for pc in range(24):
            hb = pc % 2
            stg = X[:, 2 * hb:2 * hb + 2, :].rearrange("p a b -> p (a b)").rearrange("p (k c) -> p k c", k=8)
            stgb = [X_B[2 * hb], X_B[2 * hb + 1]]
            P.dma(sp, stg, w_ada_d[:, 256 * pc:256 * (pc + 1)].rearrange("(k p) c -> p k c", p=128),
                  writes=stgb, owner=stgb[0])
            bank, bankb = next_bank()
            for ct in range(2):
                for kc in range(8):
                    mm(bank[:, 2 * ct:2 * ct + 2], stg[:, kc, 128 * ct:128 * (ct + 1)], cact[:, kc, :],
                       kc == 0, kc == 7, stgb + [CACT_B], [bankb], skip=True)
            copy_op(dve, modraw[:, 2 * pc:2 * (pc + 1), :],
                    bank[:, 0:4].rearrange("p (a b) -> p a b", a=2), [bankb], [MODRAW_B])
        for b in range(NSEQ):
            tt_op(dve, modT[:, :, b], modraw[:, :, b], b_adaT[:], ALU.add, [MODRAW_B, BADA_B], [MODT_B])
        for b in range(NSEQ):
            for which, (v, gsrc, gbuf) in enumerate(((1, g_mixT, GMIX_B), (4, g_ffnT, GFFN_B))):
                P.op(dve, lambda e, b=b, which=which, v=v, gsrc=gsrc: e.scalar_tensor_tensor(
                    out=geff[:, b, which, :], in0=modT[:, 8 * v:8 * v + 8, b], scalar=1.0, in1=gsrc[:],
                    op0=ALU.add, op1=ALU.mult), reads=[MODT_B, gbuf], writes=[GEFF_B])

        def build_gate_bc(b):
            k = 0
            for gi, v in ((0, 2), (1, 5)):
                for half in range(2):
                    bank, bankb = next_bank()
                    for q in range(4):
                        ch = half * 4 + q
                        s = k % 2
                        k += 1
                        ts_op(dve, Gm[:, s, :], ones32[:], modT[:, 8 * v + ch, b:b + 1], None, ALU.mult, None,
                              [ON_B, MODT_B], [GM_B[s]])
                        mm(bank[:, 128 * q:128 * (q + 1)], Gm[:, s, :], ident[:], True, True,
                           [GM_B[s], ID_B], [bankb], skip=True)
                    copy_op(evac_engine(), GBC[gi][:, 512 * half:512 * (half + 1)], bank[:], [bankb], [GBC_B[gi][half]])

        junk = AR[:, 22:24, :]
        JUNK_B = [AR_B[22], AR_B[23]]

        def sumsq_rstd():
            for i in range(4):
                act_op(junk, X[:, i, :].rearrange("p (a b) -> p a b", a=2), AF.Square,
                       [X_B[i]], JUNK_B + [SS_B], accum_out=ss[:, i:i + 1])
            act_op(lnv[:], ss[:], AF.Ln, [SS_B], [LNV_B], scale=1.0 / D, bias=EPS)
            act_op(rstd[:], lnv[:], AF.Exp, [LNV_B], [RSTD_B], scale=-0.5)

        def norm_to_hT(b, which, vshift):
            sumsq_rstd()
            for i in range(4):
                ts_op(dve, Dm[:, i, :], ident[:], rstd[:, i:i + 1], None, ALU.mult, None, [ID_B, RSTD_B], [DM_B[i]])
            for kc in range(8):
                bank, bankb = next_bank()
                for i in range(4):
                    mm(bank[:, 128 * i:128 * (i + 1)], X[:, i, 128 * kc:128 * (kc + 1)], Dm[:, i, :], True, True,
                       [X_B[i], DM_B[i]], [bankb], skip=True)
                eng = evac_engine()
                sc = geff[:, b, which, kc:kc + 1]
                bi = modT[:, 8 * vshift + kc, b:b + 1]
                if eng is act:
                    act_op(HT[:, kc, :], bank[:], AF.Identity, [bankb, GEFF_B, MODT_B], [HT_B[kc]], scale=sc, bias=bi)
                else:
                    ts_op(dve, HT[:, kc, :], bank[:], sc, bi, ALU.mult, ALU.add, [bankb, GEFF_B, MODT_B], [HT_B[kc]])

        def load_x(b, c):
            for i in range(4):
                r0 = c * CH + i * 128
                P.dma(sp, X[:, i, :], x_d[b, r0:r0 + 128, :], writes=[X_B[i]])

        stream = {"n": 0}

        def run_steps(steps):
            pidx = [i for i, s_ in enumerate(steps) if s_[0] is not None]
            slot_of = {}
            ptr = 0
            for i, (piece, fn) in enumerate(steps):
                if piece is not None:
                    k = pidx.index(i)
                    while ptr < len(pidx) and ptr <= k + LOOKAHEAD:
                        t, tot, gb = steps[pidx[ptr]][0]
                        sl = stream["n"] % NSLOT
                        stream["n"] += 1
                        P.dma(sp, WS[sl][:, 0:tot], t[:, :], reads=[gb], writes=[WS_B[sl]])
                        slot_of[pidx[ptr]] = sl
                        ptr += 1
                    sl = slot_of[i]
                    fn(WS[sl], WS_B[sl])
                else:
                    fn(None, None)

        QT0 = 16

        def proj_fm(piece_w, piece_b, ncols_tiles, consume):
            wv = piece_w[:, 0:2048].rearrange("p (k c) -> p k c", k=8)
            for ct in range(ncols_tiles):
                bank, bankb = next_bank()
                for kc in range(8):
                    mm(bank[:], wv[:, kc, 128 * ct:128 * (ct + 1)], HT[:, kc, :], kc == 0, kc == 7,
                       [piece_b, HT_B[kc]], [bankb])
                consume(ct, bank, bankb)

        def q_steps(nm):
            steps = []
            for i in range(2):
                def fn(w, wb, i=i):
                    def consume(ct, bank, bankb):
                        pr = 2 * i + ct
                        eng = evac_engine()
                        if eng is act:
                            act_op(AR[:, QT0 + pr, :], bank[:], AF.Copy, [bankb], [AR_B[QT0 + pr]], scale=0.125)
                        else:
                            ts_op(dve, AR[:, QT0 + pr, :], bank[:], 0.125, None, ALU.mult, None, [bankb], [AR_B[QT0 + pr]])
                    proj_fm(w, wb, 2, consume)
                steps.append((pieces[nm][i], fn))
            return steps

        def k_steps(nm, c):
            steps = []
            for i in range(2):
                def fn(w, wb, i=i):
                    def consume(ct, bank, bankb):
                        pr = 2 * i + ct
                        copy_op(evac_engine(), KT[:, pr, c * CH:(c + 1) * CH], bank[:], [bankb], [KT_B[pr][c]])
                    proj_fm(w, wb, 2, consume)
                steps.append((pieces[nm][i], fn))
            return steps

        def v_steps(nm, c):
            steps = []
            for g in range(2):
                def fn(w, wb, g=g):
                    wv = w[:, 0:2048].rearrange("p (k c) -> p k c", k=8)
                    for i in range(4):
                        blk = 4 * c + i
                        bank, bankb = next_bank()
                        for kc in range(8):
                            mm(bank[:, 0:256], HT[:, kc, 128 * i:128 * (i + 1)], wv[:, kc, :], kc == 0, kc == 7,
                               [wb, HT_B[kc]], [bankb])
                        bv = bank[:, 0:256].rearrange("p (q t d) -> p q t d", q=2, t=2)
                        copy_op(act, Vv[:, blk, 2 * g:2 * g + 2, 0:64], bv[:, :, 0, :], [bankb], [V_B[blk]])
                        copy_op(dve, Vv[:, blk, 2 * g:2 * g + 2, 128:192], bv[:, :, 1, :], [bankb], [V_B[blk]])
                steps.append((pieces[nm][g], fn))
            return steps

        def make_bg(b, c, qt0):
            plist = pieces["qsb"] + pieces["ksb"] + pieces["vsb"]
            slots = {}
            st_ = {"ptr": 0}

            def ensure(n):
                while st_["ptr"] < len(plist) and st_["ptr"] <= n + LOOKAHEAD:
                    t, tot, gb = plist[st_["ptr"]]
                    sl = stream["n"] % NSLOT
                    stream["n"] += 1
                    P.dma(sp, WS[sl][:, 0:tot], t[:, :], reads=[gb], writes=[WS_B[sl]])
                    slots[st_["ptr"]] = sl
                    st_["ptr"] += 1
                return WS[slots[n]], WS_B[slots[n]]

            def pre():
                sumsq_rstd()
                for i in range(4):
                    ts_op(dve, Dm[:, i, :], ident[:], rstd[:, i:i + 1], None, ALU.mult, None, [ID_B, RSTD_B], [DM_B[i]])

            groups = []
            for kc in range(8):
                for hf in range(2):
                    def pe_part(BG, BGb, kc=kc, hf=hf):
                        for ii in range(2):
                            i = 2 * hf + ii
                            mm(BG[:, 128 * ii:128 * (ii + 1)], X[:, i, 128 * kc:128 * (kc + 1)], Dm[:, i, :], True, True,
                               [X_B[i], DM_B[i]], [BGb], skip=True)

                    def ev_part(BG, BGb, eng, kc=kc, hf=hf):
                        sc, bi = geff[:, b, 0, kc:kc + 1], modT[:, kc, b:b + 1]
                        o = HT[:, kc, 256 * hf:256 * (hf + 1)]
                        if eng is act:
                            act_op(o, BG[:, 0:256], AF.Identity, [BGb, GEFF_B, MODT_B], [HT_B[kc]], scale=sc, bias=bi)
                        else:
                            ts_op(dve, o, BG[:, 0:256], sc, bi, ALU.mult, ALU.add, [BGb, GEFF_B, MODT_B], [HT_B[kc]])
                    groups.append((pe_part, ev_part))
            for wi, nm in enumerate(("qsb", "ksb")):
                for i in range(2):
                    for ct in range(2):
                        for hf in range(2):
                            pr = 2 * i + ct

                            def pe_part(BG, BGb, n=2 * wi + i, ct=ct, hf=hf):
                                w, wb = ensure(n)
                                wv = w[:, 0:2048].rearrange("p (k c) -> p k c", k=8)
                                for kc in range(8):
                                    mm(BG[:, 0:256], wv[:, kc, 128 * ct:128 * (ct + 1)], HT[:, kc, 256 * hf:256 * (hf + 1)],
                                       kc == 0, kc == 7, [wb, HT_B[kc]], [BGb], skip=True)

                            if nm == "qsb":
                                def ev_part(BG, BGb, eng, pr=pr, hf=hf):
                                    o = AR[:, qt0 + pr, 256 * hf:256 * (hf + 1)]
                                    if eng is act:
                                        act_op(o, BG[:, 0:256], AF.Copy, [BGb], [AR_B[qt0 + pr]], scale=0.125)
                                    else:
                                        ts_op(dve, o, BG[:, 0:256], 0.125, None, ALU.mult, None, [BGb], [AR_B[qt0 + pr]])
                            else:
                                def ev_part(BG, BGb, eng, pr=pr, hf=hf):
                                    copy_op(eng, KT[:, pr, c * CH + 256 * hf:c * CH + 256 * (hf + 1)], BG[:, 0:256],
                                            [BGb], [KT_B[pr][c]])
                            groups.append((pe_part, ev_part))
            for g in range(2):
                for i in range(4):
                    def pe_part(BG, BGb, g=g, i=i):
                        w, wb = ensure(4 + g)
                        wv = w[:, 0:2048].rearrange("p (k c) -> p k c", k=8)
                        for kc in range(8):
                            mm(BG[:, 0:256], HT[:, kc, 128 * i:128 * (i + 1)], wv[:, kc, :], kc == 0, kc == 7,
                               [wb, HT_B[kc]], [BGb], skip=True)

                    def ev_part(BG, BGb, eng, g=g, i=i):
                        blk = 4 * c + i
                        bv = BG[:, 0:256].rearrange("p (q t d) -> p q t d", q=2, t=2)
                        copy_op(eng, Vv[:, blk, 2 * g:2 * g + 2, 0:64], bv[:, :, 0, :], [BGb], [V_B[blk]])
                        copy_op(dve, Vv[:, blk, 2 * g:2 * g + 2, 128:192], bv[:, :, 1, :], [BGb], [V_B[blk]])
                    groups.append((pe_part, ev_part))
            return dict(load=lambda: load_x(b, c), pre=pre, groups=groups)

        def v3(t, col0):
            return t[:, :].rearrange("p (s c) -> p s c", s=2)[:, :, col0:CH]

        def fill(bank_i, n, first=False):
            for j in range(n):
                P.op(pe, lambda e: e.matmul(PB[bank_i][:, 384:512], lhsT=triN[:], rhs=triN[:], start=True, stop=True,
                                            skip_group_check=True),
                     reads=[TRIN_B] if (first and j == 0) else [], writes=[PB_B[bank_i]] if (first and j == 0) else [])

        def sb_attention(c, qt0, bg):
            nst = 4 * c + 4
            nf = NF_SB if bg is None else 1
            Zp, Zb = PP[1], [PB_B[2], PB_B[3]]
            Cp, Cb = PP[2], [PB_B[4], PB_B[5]]
            XPp, XPb = GBC[1], GBC_B[1]
            for p in range(4):
                Y, Yb = PB[6], PB_B[6]
                qb = AR_B[qt0 + p]

                def kbof(k):
                    return nst - 1 - k

                def col0of(k):
                    kb = kbof(k)
                    return 128 * (kb - 4 * c) if kb >= 4 * c else 0

                def pe1(k):
                    kb, c0 = kbof(k), col0of(k)
                    for s in range(2):
                        mm(Zp[:, CH * s + c0:CH * (s + 1)], KT[64 * s:64 * s + 64, p, 128 * kb:128 * (kb + 1)],
                           AR[64 * s:64 * s + 64, qt0 + p, c0:CH], True, True,
                           [KT_B[p][kb // 4], qb], [Zb[s]])

                def act1(k):
                    c0 = col0of(k)
                    e, eb = EE3[k % 3], EE3_B[k % 3]
                    act_op(v3(e, c0), v3(Zp, c0), AF.Exp, Zb, eb)
                    if kbof(k) >= 4 * c:
                        n = CH - c0
                        P.op(pool, lambda en, e=e, c0=c0, n=n: en.affine_select(
                            out=v3(e, c0), in_=v3(e, c0), pattern=[[0, 2], [1, n]], compare_op=ALU.is_gt, fill=0.0,
                            base=0, channel_multiplier=-1), reads=eb, writes=eb)

                def act2(k):
                    c0 = col0of(k)
                    act_op(v3(SPP[k % 2], c0), v3(EE3[k % 3], c0), AF.Ln, EE3_B[k % 3], SPP_B[k % 2], bias=1.0)

                def pe2(k):
                    c0 = col0of(k)
                    for s in range(2):
                        mm(Cp[:, CH * s + c0:CH * (s + 1)], triN[:], SPP[k % 2][:, CH * s + c0:CH * (s + 1)], k == 0, True,
                           [TRIN_B, SPP_B[k % 2][s]], [Cb[s]], skip=True)

                def act3(k):
                    c0 = col0of(k)
                    act_op(v3(XPp, c0), v3(Cp, c0), AF.Exp, Cb, XPb)

                def pe3(k):
                    c0 = col0of(k)
                    for s in range(2):
                        mm(Cp[:, CH * s + c0:CH * (s + 1)], cmpN[:], SPP[k % 2][:, CH * s + c0:CH * (s + 1)], False, True,
                           [CMPN_B, SPP_B[k % 2][s]], [Cb[s]], skip=True)

                def dve_w(k):
                    c0 = col0of(k)
                    for s in range(2):
                        sl = slice(CH * s + c0, CH * (s + 1))
                        tt_op(dve, WW[:, sl], EE3[k % 3][:, sl], XPp[:, sl], ALU.mult,
                              [EE3_B[k % 3][s], XPb[s]], [WW_B[s]])

                def pe4(k):
                    kb, c0 = kbof(k), col0of(k)
                    for s in range(2):
                        vc = 192 * p + 128 * s
                        mm(Y[64 * s:64 * s + 64, c0:CH], V[:, kb, vc:vc + 64], WW[:, CH * s + c0:CH * (s + 1)],
                           k == 0, k == nst - 1, [V_B[kb], WW_B[s]], [Yb], skip=True)

                fill(7, 1, first=True)
                if bg is not None and p == 0:
                    bg["load"]()
                pe1(0)
                act1(0)
                pe1(1)
                act2(0)
                for k in range(nst):
                    if bg is not None and (bg["groups"] or not bg.get("drained")):
                        nf = 0
                        if not bg["groups"]:
                            bg["drained"] = True
                            fill(7, 1, first=True)
                    else:
                        nf = NF_SB
                    pe2(k)
                    if k > 0:
                        pe4(k - 1)
                    fill(7, nf)
                    if k + 1 < nst:
                        act1(k + 1)
                    if k + 2 < nst:
                        pe1(k + 2)
                        fill(7, nf)
                    act3(k)
                    if k < nst - 1:
                        pe3(k)
                        fill(7, nf)
                    grp = None
                    if bg is not None:
                        if p == 0 and k == 2:
                            bg["pre"]()
                        elif (p > 0 or k > 2) and bg["groups"]:
                            grp = bg["groups"].pop(0)
                            grp[0](PB[7], PB_B[7])
                    dve_w(k)
                    if grp is not None:
                        grp[1](PB[7], PB_B[7], dve)
                    if k + 1 < nst:
                        act2(k + 1)
                    if k == 4 and c >= 3:
                        emit_casts(5)
                pe4(nst - 1)
                copy_op(dve, YSB[:, p, c * CH:(c + 1) * CH], Y[:], [Yb], [YSB_B[p][c]])
            if bg is not None:
                while bg["groups"]:
                    grp = bg["groups"].pop(0)
                    bank, bankb = next_bank()
                    grp[0](bank, bankb)
                    grp[1](bank, bankb, evac_engine())

        def sq_of(hh):
            if hh % 2 == 0:
                return AR[:, 20 + hh // 2, :], AR_B[20 + hh // 2]
            return HT[:, 4 + hh // 2, :], HT_B[4 + hh // 2]

        def fox_prep_thunks(c):
            st8 = {}

            def prepA(i):
                def fn(w_, wb_):
                    bank, bankb = next_bank()
                    for kc in range(8):
                        mm(bank[:, 0:8], HT[:, kc, 128 * i:128 * (i + 1)], wf[:, kc, :], kc == 0, kc == 7,
                           [HT_B[kc], WF_B], [bankb])
                    tt_op(dve, ft[:], bank[:, 0:8], bfor[:], ALU.add, [bankb, BFOR_B], [FT_B])
                    act_op(fe[:], ft[:], AF.Exp, [FT_B], [FE_B], scale=-1.0)
                    act_op(fl[:, i, :], fe[:], AF.Ln, [FE_B], [FL_B[i]], bias=1.0)
                return fn

            def prepB(i):
                def fn(w_, wb_):
                    blk = 4 * c + i
                    bank2, bank2b = next_bank()
                    mm(bank2[:, 0:8], triIN[:], fl[:, i, :], True, blk == 0, [TRIIN_B, FL_B[i]], [bank2b], skip=True)
                    if blk > 0:
                        mm(bank2[:, 0:8], selL[:], ckT[:, blk - 1, :], False, True, [SELL_B, CK_B[blk - 1]], [bank2b], skip=True)
                    copy_op(dve, ckT[:, blk, :], bank2[:, 0:8], [bank2b], [CK_B[blk]])
                return fn

            def biasA(w_, wb_):
                bank3, bank3b = next_bank()
                mm(bank3[:, 0:8], selL[:], ckT[:, 4 * c + 1, :], True, True, [SELL_B, CK_B[4 * c + 1]], [bank3b], skip=True)
                copy_op(dve, crefbc[:], bank3[:, 0:8], [bank3b], [CREF_B])

            def biasB(w_, wb_):
                nkb = 4 * c + 4
                for hh in range(8):
                    ts_op(dve, biasT[:, hh, 0:nkb], ckT[:, 0:nkb, hh], crefbc[:, hh:hh + 1], -1.0, ALU.subtract, ALU.mult,
                          CK_B[0:nkb] + [CREF_B], [BIAS_B])

            def shiftq(w_, wb_):
                k = 0
                for hh in range(8):
                    bank4, bank4b = next_bank()
                    for i in range(4):
                        mm(bank4[:, 128 * i:128 * (i + 1)], ckT[:, 4 * c + i, hh:hh + 1].broadcast_to([128, 128]), ident[:],
                           True, True, [CK_B[4 * c + i], ID_B], [bank4b], skip=True)
                    dst, dstb = sq_of(hh)
                    ts_op(dve, dst, bank4[:], -1.0, crefbc[:, hh:hh + 1], ALU.mult, ALU.add,
                          [bank4b, CREF_B], [dstb])

            def seq(*fns):
                def fn(w_, wb_):
                    for f_ in fns:
                        f_(w_, wb_)
                return fn

            return [seq(prepA(0), prepA(1)), seq(prepA(2), prepA(3), prepB(0)), seq(prepB(1)), seq(prepB(2)),
                    seq(prepB(3)), seq(biasA), seq(biasB)], shiftq

        def fox_attention(c):
            nst = 4 * c + 4
            for p in range(4):
                ybanks = (6, 7) if p % 2 == 0 else (4, 5)
                heads = []
                for s in range(2):
                    r0 = 64 * s
                    heads.append(dict(
                        s=s, r0=r0, hh=2 * p + s,
                        qb=AR_B[QT0 + p],
                        vc=192 * p + 64 * s,
                        Z=[PB[2 + s], PB[s]], Zb=[PB_B[2 + s], PB_B[s]],
                        Y=PB[ybanks[s]], Yb=PB_B[ybanks[s]],
                        pb=[(SPP[0][:, CH * s:CH * (s + 1)], SPP_B[0][s]), (SPP[1][:, CH * s:CH * (s + 1)], SPP_B[1][s])]))

                def col0of(kb):
                    return 128 * (kb - 4 * c) if kb >= 4 * c else 0

                def pe1(h, kb):
                    c0 = col0of(kb)
                    r0 = h["r0"]
                    mm(h["Z"][kb % 2][:, c0:CH], KT[r0:r0 + 64, p, 128 * kb:128 * (kb + 1)], AR[r0:r0 + 64, QT0 + p, c0:CH],
                       True, True, [KT_B[p][kb // 4], h["qb"]], [h["Zb"][kb % 2]])

                def pe1s(h, kb):
                    c0 = col0of(kb)
                    sq, sqb = sq_of(h["hh"])
                    zt = h["Z"][kb % 2]
                    tt_op(dve, zt[:, c0:CH], zt[:, c0:CH], sq[:, c0:CH], ALU.subtract,
                          [h["Zb"][kb % 2], sqb], [h["Zb"][kb % 2]])

                for kb0 in range(2):
                    for h in heads:
                        pe1(h, kb0)
                    for h in heads:
                        pe1s(h, kb0)
                for kb in range(nst):
                    first, last = kb == 0, kb == nst - 1
                    c0 = col0of(kb)
                    n = CH - c0
                    for h in heads:
                        pt, ptb = h["pb"][kb % 2]
                        act_op(pt[:, c0:CH], h["Z"][kb % 2][:, c0:CH], AF.Exp, [h["Zb"][kb % 2], BIAS_B], [ptb],
                               bias=biasT[:, h["hh"], kb:kb + 1])
                        if kb >= 4 * c:
                            P.op(pool, lambda e, pt=pt, c0=c0, n=n: e.affine_select(
                                out=pt[:, c0:CH], in_=pt[:, c0:CH], pattern=[[1, n]], compare_op=ALU.is_ge, fill=0.0,
                                base=0, channel_multiplier=-1), reads=[ptb], writes=[ptb])
                    if kb + 2 < nst:
                        for h in heads:
                            pe1(h, kb + 2)
                        for h in heads:
                            pe1s(h, kb + 2)
                        fill(4, NF_FX)
                    for h in heads:
                        pt, ptb = h["pb"][kb % 2]
                        mm(h["Y"][:, c0:CH], V[:, kb, h["vc"]:h["vc"] + 128], pt[:, c0:CH], first, last,
                           [V_B[kb], ptb], [h["Yb"]], skip=True)
                    fill(4, NF_FX)
                for h in heads:
                    s = h["s"]
                    yr = slice(0, 64) if s == 0 else slice(64, 128)
                    dr = slice(64, 128) if s == 0 else slice(0, 64)
                    P.op(dve, lambda e, h=h, s=s, dr=dr: e.reciprocal(out=XC[s][dr, :], in_=h["Y"][dr, :]),
                         reads=[h["Yb"]], writes=[XC_B[s]])
                    tt_op(dve, HT[yr, p, :], h["Y"][yr, :], XC[s][dr, :], ALU.mult, [h["Yb"], XC_B[s]], [HT_B[p]])

        def gate_steps():
            steps = []
            for i in range(8):
                def fn(w, wb, i=i):
                    def consume(ct, bank, bankb):
                        j = 2 * i + ct
                        act_op(AR[:, j, :], bank[:], AF.Sigmoid, [bankb, BGATE_B], [AR_B[j]], bias=bgateT[:, j:j + 1])
                    proj_fm(w, wb, 2, consume)
                steps.append((pieces["gate"][i], fn))
            return steps

        def merge_steps(c):
            steps = []
            for i in range(4):
                def fn(w, wb, i=i):
                    wv = w[:, 0:2048].rearrange("p (k c) -> p k c", k=8)
                    for ct in range(2):
                        j = 2 * i + ct
                        b1, b1b = next_bank()
                        for kc in range(4):
                            mm(b1[:], wv[:, kc, 128 * ct:128 * (ct + 1)], YSB[:, kc, c * CH:(c + 1) * CH], kc == 0, kc == 3,
                               [wb, YSB_B[kc][c]], [b1b])
                        b2, b2b = next_bank()
                        for kc in range(4):
                            mm(b2[:], wv[:, 4 + kc, 128 * ct:128 * (ct + 1)], HT[:, kc, :], kc == 0, kc == 3,
                               [wb, HT_B[kc]], [b2b])
                        t1, t1b = (E_[ct], E_B[ct])
                        t2, t2b = (XC[ct], XC_B[ct])
                        tt_op(dve, t1[:], AR[:, j, :], b1[:], ALU.mult, [AR_B[j], b1b], [t1b])
                        tt_op(dve, t2[:], AR[:, 8 + j, :], b2[:], ALU.mult, [AR_B[8 + j], b2b], [t2b])
                        tt_op(pool if ct == 0 else dve, AR[:, 16 + j, :], t1[:], t2[:], ALU.add, [t1b, t2b], [AR_B[16 + j]])
                steps.append((pieces["merge"][i], fn))
            return steps

        ALT_T = [(SPP[0][:, :].bitcast(F32), SPP_B[0]), (SPP[1][:, :].bitcast(F32), SPP_B[1]),
                 (WW[:, :].bitcast(F32), WW_B)]

        def resid_update(i, cols, bank, bankb, gi, half_b, k, alt=False):
            if alt:
                t, tbs = ALT_T[k % 3]
            else:
                t, tb = (E_[k % 2], E_B[k % 2]) if (k // 2) % 2 == 0 else (XC[k % 2], XC_B[k % 2])
                tbs = [tb]
            n = cols.stop - cols.start
            tt_op(dve, t[:, 0:n], bank[:, 0:n], GBC[gi][:, cols], ALU.mult, [bankb, half_b], tbs)
            tt_op(pool if k % 2 == 0 else dve, X[:, i, cols], X[:, i, cols], t[:, 0:n], ALU.add, [X_B[i]] + tbs, [X_B[i]])

        def out_steps():
            steps = []
            for q in range(4):
                def fn(w, wb, q=q):
                    wv = w[:, 0:2048].rearrange("p (k c) -> p k c", k=8)
                    for i in range(4):
                        bank, bankb = next_bank()
                        for kc in range(8):
                            mm(bank[:, 0:256], AR[:, 16 + kc, 128 * i:128 * (i + 1)], wv[:, kc, :], kc == 0, kc == 7,
                               [wb, AR_B[16 + kc]], [bankb])
                        resid_update(i, slice(256 * q, 256 * (q + 1)), bank, bankb, 0, GBC_B[0][q // 2], i)
                steps.append((pieces["out"][q], fn))
            return steps

        def gu_steps():
            steps = []
            for j in range(NJ):
                def fn(w, wb, j=j):
                    wv = w[:, 0:2048].rearrange("p (t k c) -> p t k c", t=2, k=8)
                    bg, bgb = next_bank()
                    for kc in range(8):
                        mm(bg[:], wv[:, 0, kc, :], HT[:, kc, :], kc == 0, kc == 7, [wb, HT_B[kc]], [bgb])
                    bu, bub = next_bank()
                    for kc in range(8):
                        mm(bu[:], wv[:, 1, kc, :], HT[:, kc, :], kc == 0, kc == 7, [wb, HT_B[kc]], [bub])
                    t, tb = (E_[j % 2], E_B[j % 2])
                    act_op(t[:], bg[:], AF.Silu, [bgb], [tb])
                    tt_op(dve, AR[:, j, :], t[:], bu[:], ALU.mult, [tb, bub], [AR_B[j]])
                steps.append((pieces["gu"][j], fn))
            return steps

        def make_pnorm(b, cn):
            def load(wv):
                def fn():
                    for ii in range(2):
                        r0 = cn * CH + (2 * wv + ii) * 128
                        P.dma(sp, EE[ii][:, :], x_d[b, r0:r0 + 128, :], writes=EE_B[ii], owner=EE_B[ii][0])
                return fn

            def partA(wv):
                def fn():
                    for ii in range(2):
                        i = 2 * wv + ii
                        act_op(junk, EE[ii][:, :].rearrange("p (a b) -> p a b", a=2), AF.Square,
                               EE_B[ii], JUNK_B + [SS_B], accum_out=ss[:, i:i + 1])
                    act_op(lnv[:, 2 * wv:2 * wv + 2], ss[:, 2 * wv:2 * wv + 2], AF.Ln, [SS_B], [LNV_B], scale=1.0 / D, bias=EPS)
                    act_op(rstd[:, 2 * wv:2 * wv + 2], lnv[:, 2 * wv:2 * wv + 2], AF.Exp, [LNV_B], [RSTD_B], scale=-0.5)
                    for ii in range(2):
                        i = 2 * wv + ii
                        ts_op(dve, Dm[:, i, :], ident[:], rstd[:, i:i + 1], None, ALU.mult, None, [ID_B, RSTD_B], [DM_B[i]])
                return fn

            def partB(wv):
                def fn():
                    for kc in range(8):
                        bank, bankb = next_bank()
                        for ii in range(2):
                            i = 2 * wv + ii
                            mm(bank[:, 128 * ii:128 * (ii + 1)], EE[ii][:, 128 * kc:128 * (kc + 1)], Dm[:, i, :], True, True,
                               EE_B[ii] + [DM_B[i]], [bankb], skip=True)
                        eng = evac_engine()
                        sc = geff[:, b, 0, kc:kc + 1]
                        bi = modT[:, kc, b:b + 1]
                        o = HT[:, kc, 256 * wv:256 * (wv + 1)]
                        if eng is act:
                            act_op(o, bank[:, 0:256], AF.Identity, [bankb, GEFF_B, MODT_B], [HT_B[kc]], scale=sc, bias=bi)
                        else:
                            ts_op(dve, o, bank[:, 0:256], sc, bi, ALU.mult, ALU.add, [bankb, GEFF_B, MODT_B], [HT_B[kc]])
                return fn

            return {(0, 0): [load(0)], (0, 1): [partA(0)], (0, 3): [partB(0), load(1)], (0, 5): [partA(1)],
                    (1, 1): [partB(1)]}

        def down_steps(hooks):
            steps = []
            ACC = [(PB[4], PB_B[4]), (PB[5], PB_B[5]), (PB[6], PB_B[6]), (PB[7], PB_B[7])]
            for hf in range(2):
                for g, (j0, j1) in enumerate(JG):
                    def fn(w, wb, hf=hf, g=g, j0=j0, j1=j1):
                        for hk in hooks.get((hf, g), []):
                            hk()
                        wv = w[:, 0:(j1 - j0) * 512].rearrange("p (j c) -> p j c", j=j1 - j0)
                        for i in range(4):
                            for j in range(j0, j1):
                                mm(ACC[i][0][:], AR[:, j, 128 * i:128 * (i + 1)], wv[:, j - j0, :], j == 0, j == NJ - 1,
                                   [wb, AR_B[j]], [ACC[i][1]])
                        if j1 == NJ:
                            for i in range(4):
                                resid_update(i, slice(512 * hf, 512 * (hf + 1)), ACC[i][0], ACC[i][1], 1, GBC_B[1][hf], i,
                                             alt=True)
                    steps.append((pieces["down"][hf][g], fn))
            return steps

        def final_norm_store(b, c, prefetch_next):
            sumsq_rstd()
            for i in range(4):
                if i < 2:
                    stg, stgb = EE[i][:, :], EE_B[i]
                else:
                    stg, stgb = X[:, i, :], [X_B[i]]
                P.op(dve, lambda e, i=i, stg=stg: e.scalar_tensor_tensor(
                    out=stg, in0=X[:, i, :], scalar=rstd[:, i:i + 1], in1=gfin[:],
                    op0=ALU.mult, op1=ALU.mult), reads=[X_B[i], RSTD_B, GFIN_B], writes=stgb)
                r0 = c * CH + i * 128
                P.dma(pool, out_d[b, r0:r0 + 128, :], stg, reads=stgb, owner=stgb[0], is_store=True)
                if prefetch_next and i < 2:
                    r1 = (c + 1) * CH + i * 128
                    P.dma(pool, X[:, i, :], x_d[b, r1:r1 + 128, :], writes=[X_B[i]])
            if prefetch_next:
                for i in (2, 3):
                    r1 = (c + 1) * CH + i * 128
                    P.dma(pool, X[:, i, :], x_d[b, r1:r1 + 128, :], writes=[X_B[i]])

        for b in range(NSEQ):
            steps = [(None, lambda w, wb: (load_x(b, 0), norm_to_hT(b, 0, 0)))]
            steps += q_steps("qsb") + k_steps("ksb", 0) + v_steps("vsb", 0)
            run_steps(steps)
            for c in range(NCH):
                qt0 = QT0 if c % 2 == 0 else 0
                qt0n = QT0 if (c + 1) % 2 == 0 else 0
                bg = make_bg(b, c + 1, qt0n) if c + 1 < NCH else None
                sb_attention(c, qt0, bg)
            emit_casts(10000)
            build_gate_bc(b)
            all_steps = []
            for c in range(NCH):
                if c == 0:
                    steps = [(None, lambda w, wb, c=c: (load_x(b, c), norm_to_hT(b, 0, 0)))]
                else:
                    steps = []
                steps += q_steps("qfx") + k_steps("kfx", c) + v_steps("vfx", c)
                th, shq = fox_prep_thunks(c)
                gs = gate_steps()
                for gi in range(8):
                    if gi < len(th):
                        steps.append((None, th[gi]))
                    steps.append(gs[gi])
                steps.append((None, shq))
                steps += [(None, lambda w, wb, c=c: fox_attention(c))]
                steps += merge_steps(c) + out_steps()
                steps += [(None, lambda w, wb: norm_to_hT(b, 1, 3))]
                steps += gu_steps() + down_steps(make_pnorm(b, c + 1) if c + 1 < NCH else {})
                steps += [(None, lambda w, wb, c=c: final_norm_store(b, c, c + 1 < NCH))]
                all_steps += steps
            run_steps(all_steps)

        P.wait_all(sp, X_B + EE_B[0] + EE_B[1])
        P.emit()
    return nc


_NC_CACHE = {}


def kernel(x, c, w_ada, b_ada, g_mix, w_in, b_forget, b_gate, w_branch_sb, w_branch_fox,
           w_out, g_ffn, w_ffn_gate, w_ffn_up, w_ffn_down, g_final):
    f = lambda a: np.ascontiguousarray(np.asarray(a, dtype=np.float32))
    x = f(x)
    c = f(c)
    if "nc" not in _NC_CACHE:
        _NC_CACHE["nc"] = build_nc()
    nc = _NC_CACHE["nc"]

    def featT(v, nch):
        return f(np.asarray(v, dtype=np.float32).reshape(nch, 128).T)

    shared = {
        "w_ada": f(w_ada[0]),
        "b_adaT": featT(b_ada[0], 48),
        "g_mixT": featT(g_mix[0], 8),
        "g_ffnT": featT(g_ffn[0], 8),
        "gfin_bc": f(np.broadcast_to(np.asarray(g_final, dtype=np.float32)[None, :], (128, D))),
        "w_in": f(w_in[0]),
        "bfor_bc": f(np.broadcast_to(np.asarray(b_forget[0], dtype=np.float32)[None, :], (128, 8))),
        "b_gateT": featT(b_gate[0], 16),
        "w_bsb": f(w_branch_sb[0]),
        "w_bfx": f(w_branch_fox[0]),
        "w_out": f(w_out[0]),
        "w_fg": f(w_ffn_gate[0]),
        "w_fu": f(w_ffn_up[0]),
        "w_fd": f(w_ffn_down[0]),
    }
    in_maps = []
    for i in range(NCORES):
        m = dict(shared)
        m["x"] = x[NSEQ * i:NSEQ * (i + 1)]
        cc = c[NSEQ * i:NSEQ * (i + 1)]
        m["cT"] = f(cc.reshape(NSEQ, 8, 128).transpose(2, 1, 0))
        in_maps.append(m)
    res = run_bass_kernel_spmd(nc, in_maps, core_ids=list(range(NCORES)))
    out = np.concatenate([np.asarray(r["out"]) for r in res.results], axis=0)
    return out.astype(np.float32, copy=False)
```

```python
import numpy as np
from contextlib import ExitStack
import concourse.bass as bass
import concourse.mybir as mybir
from concourse.bass_utils import run_bass_kernel_spmd

F32 = mybir.dt.float32
BF16 = mybir.dt.bfloat16
AF = mybir.ActivationFunctionType
ALU = mybir.AluOpType

SEM_ROLL = 30000
SAME_ENGINE_SYNC = True

D = 1024
S = 4096
NSEQ = 2
CH = 512
NCH = S // CH
DFF = 2816
NJ = DFF // 128
EPS = 1e-6
NCORES = 8
NSLOT = 3
SLOT_ELEMS = 2048
LOOKAHEAD = NSLOT - 1
NF_SB = 4
NF_FX = 0


class Buf:
    __slots__ = ("name", "w", "r", "dsem_in", "dcnt_in", "dsem_out", "dcnt_out")

    def __init__(self, name):
        self.name = name
        self.w = {}
        self.r = {}
        self.dsem_in = None
        self.dcnt_in = 0
        self.dsem_out = None
        self.dcnt_out = 0


class Eng:
    def __init__(self, prog, name, is_pe=False):
        self.prog = prog
        self.name = name
        self.is_pe = is_pe
        self.ops = []
        self.count = 0
        self.sem = prog.new_sem(name)
        self.waited = {}

    def roll(self):
        if self.count >= SEM_ROLL:
            self.sem = self.prog.new_sem(self.name)
            self.count = 0


class Prog:
    def __init__(self, nc, stack):
        self.nc = nc
        self.stack = stack
        self.nsem = 0
        self.pe = Eng(self, "pe", True)
        self.act = Eng(self, "act")
        self.dve = Eng(self, "dve")
        self.pool = Eng(self, "pool")
        self.sp = Eng(self, "sp")

    def new_sem(self, name):
        self.nsem += 1
        return self.stack.enter_context(self.nc.semaphore(f"s{self.nsem}_{name}"))

    def _deps(self, eng, reads, writes):
        deps = {}
        for b in reads:
            for s, v in b.w.items():
                if deps.get(s, 0) < v:
                    deps[s] = v
        for b in writes:
            for d in (b.w, b.r):
                for s, v in d.items():
                    if deps.get(s, 0) < v:
                        deps[s] = v
        waits = []
        for s, v in deps.items():
            if s is eng.sem and (eng.is_pe or not SAME_ENGINE_SYNC):
                continue
            if eng.waited.get(s, 0) < v:
                eng.waited[s] = v
                waits.append((s, v))
        return waits

    def op(self, eng, fn, reads=(), writes=()):
        eng.roll()
        waits = self._deps(eng, reads, writes)
        eng.count += 1
        tok = (eng.sem, eng.count)
        eng.ops.append((waits, fn, tok[0], 1))
        for b in reads:
            if b.r.get(tok[0], 0) < tok[1]:
                b.r[tok[0]] = tok[1]
        for b in writes:
            b.w = {tok[0]: tok[1]}
            b.r = {}
        return tok

    def dma(self, eng, out_ap, in_ap, reads=(), writes=(), owner=None, is_store=False):
        waits = self._deps(eng, reads, writes)
        if owner is None:
            owner = writes[0] if writes else reads[0]
        if is_store:
            if owner.dsem_out is None:
                owner.dsem_out = self.new_sem("do_" + owner.name)
            owner.dcnt_out += 16
            tok = (owner.dsem_out, owner.dcnt_out)
        else:
            if owner.dsem_in is None:
                owner.dsem_in = self.new_sem("di_" + owner.name)
            owner.dcnt_in += 16
            tok = (owner.dsem_in, owner.dcnt_in)
        fn = (lambda e, o=out_ap, i=in_ap: e.dma_start(out=o, in_=i))
        eng.ops.append((waits, fn, tok[0], 16))
        for b in reads:
            if b.r.get(tok[0], 0) < tok[1]:
                b.r[tok[0]] = tok[1]
        for b in writes:
            b.w = {tok[0]: tok[1]}
            b.r = {}
        return tok

    def wait_all(self, eng, bufs):
        waits = self._deps(eng, (), bufs)
        eng.ops.append((waits, None, None, 0))

    def emit(self):
        nc = self.nc
        with nc.Block() as block:
            def run(e, eng):
                for waits, fn, sem, inc in eng.ops:
                    for s, v in waits:
                        e.wait_ge(s, v)
                    if fn is not None:
                        fn(e).then_inc(sem, inc)

            @block.tensor
            def _(e):
                run(e, self.pe)

            @block.scalar
            def _(e):
                run(e, self.act)

            @block.vector
            def _(e):
                run(e, self.dve)

            @block.gpsimd
            def _(e):
                run(e, self.pool)

            @block.sync
            def _(e):
                run(e, self.sp)


def build_nc():
    nc = bass.Bass("TRN2", target_bir_lowering=False)

    def din(name, shape):
        return nc.dram_tensor(name, shape, F32, kind="ExternalInput").ap()

    x_d = din("x", [NSEQ, S, D])
    cT_d = din("cT", [128, 8, NSEQ])
    w_ada_d = din("w_ada", [D, 6 * D])
    b_adaT_d = din("b_adaT", [128, 48])
    g_mixT_d = din("g_mixT", [128, 8])
    g_ffnT_d = din("g_ffnT", [128, 8])
    gfin_d = din("gfin_bc", [128, D])
    w_in_d = din("w_in", [D, 5128])
    bfor_d = din("bfor_bc", [128, 8])
    bgateT_d = din("b_gateT", [128, 16])
    w_bsb_d = din("w_bsb", [512, D])
    w_bfx_d = din("w_bfx", [512, D])
    w_out_d = din("w_out", [D, D])
    w_fg_d = din("w_fg", [D, DFF])
    w_fu_d = din("w_fu", [D, DFF])
    w_fd_d = din("w_fd", [DFF, D])
    out_d = nc.dram_tensor("out", [NSEQ, S, D], F32, kind="ExternalOutput").ap()

    with ExitStack() as st:
        P = Prog(nc, st)
        pe, act, dve, pool, sp = P.pe, P.act, P.dve, P.pool, P.sp

        def sb(name, shape, dt):
            return st.enter_context(nc.sbuf_tensor("sb_" + name, shape, dt))

        def ps(name):
            return st.enter_context(nc.psum_tensor(name, [128, 512], F32))

        KT = sb("KT", [128, 4, S], BF16)
        KT_B = [[Buf(f"KT{p}_{c}") for c in range(NCH)] for p in range(4)]
        V = sb("V", [128, 32, 768], BF16)
        V_B = [Buf(f"V{b}") for b in range(32)]
        YSB = sb("YSB", [128, 4, S], BF16)
        YSB_B = [[Buf(f"YSB{p}_{c}") for c in range(NCH)] for p in range(4)]
        X = sb("X", [128, 4, D], F32)
        X_B = [Buf(f"X{i}") for i in range(4)]
        HT = sb("HT", [128, 8, CH], BF16)
        HT_B = [Buf(f"HT{k}") for k in range(8)]
        AR = sb("AR", [128, 24, CH], BF16)
        AR_B = [Buf(f"AR{k}") for k in range(24)]
        WS = [sb(f"WS{i}", [128, SLOT_ELEMS], BF16) for i in range(NSLOT)]
        WS_B = [Buf(f"WS{i}") for i in range(NSLOT)]
        EE = [sb(f"ee{k}", [128, 2 * CH], F32) for k in range(2)]
        EE_B = [[Buf(f"ee{k}_{s}") for s in range(2)] for k in range(2)]
        E_ = [EE[0][:, CH * s:CH * (s + 1)] for s in range(2)]
        E_B = EE_B[0]
        XC = [EE[1][:, CH * s:CH * (s + 1)] for s in range(2)]
        XC_B = EE_B[1]
        SPP = [sb(f"spp{k}", [128, 2 * CH], BF16) for k in range(2)]
        SPP_B = [[Buf(f"spp{k}_{s}") for s in range(2)] for k in range(2)]
        WW = sb("ww", [128, 2 * CH], BF16)
        WW_B = [Buf(f"ww{s}") for s in range(2)]
        GBC = [sb(f"gbc{g}", [128, D], F32) for g in range(2)]
        GBC_B = [[Buf(f"gbc{g}_{h}") for h in range(2)] for g in range(2)]
        EE3 = [EE[0], EE[1], GBC[0]]
        EE3_B = [EE_B[0], EE_B[1], GBC_B[0]]
        gfin = sb("gfin", [128, D], F32)
        GFIN_B = Buf("gfin")
        ident = sb("ident", [128, 128], F32)
        ID_B = Buf("ident")
        ones32 = sb("ones32", [128, 128], F32)
        ON_B = Buf("ones32")
        triIN = sb("triIN", [128, 128], F32)
        TRIIN_B = Buf("triIN")
        selL = sb("selL", [128, 128], F32)
        SELL_B = Buf("selL")
        triN = sb("triN", [128, 128], BF16)
        TRIN_B = Buf("triN")
        cmpN = sb("cmpN", [128, 128], BF16)
        CMPN_B = Buf("cmpN")
        tmpb = sb("tmpb", [128, 128], BF16)
        TMPB_B = Buf("tmpb")
        selS = sb("selS", [128, 128], BF16)
        SELS_B = Buf("selS")
        Dm = sb("Dm", [128, 4, 128], F32)
        DM_B = [Buf(f"Dm{i}") for i in range(4)]
        Gm = sb("Gm", [128, 2, 128], F32)
        GM_B = [Buf(f"Gm{i}") for i in range(2)]
        ckT = sb("ckT", [128, 32, 8], F32)
        CK_B = [Buf(f"ck{b}") for b in range(32)]
        biasT = sb("biasT", [128, 8, 32], F32)
        BIAS_B = Buf("biasT")
        crefbc = sb("crefbc", [128, 8], F32)
        CREF_B = Buf("cref")
        ft = sb("ft", [128, 8], F32)
        FT_B = Buf("ft")
        fe = sb("fe", [128, 8], F32)
        FE_B = Buf("fe")
        fl = sb("fl", [128, 4, 8], F32)
        FL_B = [Buf(f"fl{i}") for i in range(4)]
        ss = sb("ss", [128, 4], F32)
        SS_B = Buf("ss")
        lnv = sb("lnv", [128, 4], F32)
        LNV_B = Buf("lnv")
        rstd = sb("rstd", [128, 4], F32)
        RSTD_B = Buf("rstd")
        cT = sb("cT", [128, 8, NSEQ], F32)
        CT_B = Buf("cT")
        cact = sb("cact", [128, 8, NSEQ], F32)
        CACT_B = Buf("cact")
        csig = sb("csig", [128, 8, NSEQ], F32)
        CSIG_B = Buf("csig")
        modraw = sb("modraw", [128, 48, NSEQ], F32)
        MODRAW_B = Buf("modraw")
        modT = sb("modT", [128, 48, NSEQ], F32)
        MODT_B = Buf("modT")
        b_adaT = sb("b_adaT", [128, 48], F32)
        BADA_B = Buf("b_adaT")
        g_mixT = sb("g_mixT", [128, 8], F32)
        GMIX_B = Buf("g_mixT")
        g_ffnT = sb("g_ffnT", [128, 8], F32)
        GFFN_B = Buf("g_ffnT")
        geff = sb("geff", [128, NSEQ, 2, 8], F32)
        GEFF_B = Buf("geff")
        bfor = sb("bfor", [128, 8], F32)
        BFOR_B = Buf("bfor")
        bgateT = sb("bgateT", [128, 16], F32)
        BGATE_B = Buf("bgateT")
        wf = sb("wf", [128, 8, 8], BF16)
        WF_B = Buf("wf")

        PP = [st.enter_context(nc.psum_tensor(f"pp{j}", [128, 2 * CH], F32)) for j in range(4)]
        PB = [PP[i // 2][:, CH * (i % 2):CH * (i % 2 + 1)] for i in range(8)]
        PB_B = [Buf(f"pb{i}") for i in range(8)]
        rot = {"i": 0}

        def next_bank():
            k = (0, 1, 2, 3)[rot["i"] % 4]
            rot["i"] += 1
            return PB[k], PB_B[k]

        def mm(out_ap, lhsT, rhs, start, stop, reads, writes, skip=False):
            if skip:
                P.op(pe, lambda e: e.matmul(out_ap, lhsT=lhsT, rhs=rhs, start=start, stop=stop,
                                            skip_group_check=True), reads=reads, writes=writes)
            else:
                P.op(pe, lambda e: e.matmul(out_ap, lhsT=lhsT, rhs=rhs, start=start, stop=stop),
                     reads=reads, writes=writes)

        def act_op(out, in_, func, reads, writes, **kw):
            P.op(act, lambda e: e.activation(out=out, in_=in_, func=func, **kw), reads=reads, writes=writes)

        def ts_op(eng, out, in0, s1, s2, op0, op1, reads, writes):
            if s2 is None:
                P.op(eng, lambda e: e.tensor_scalar(out=out, in0=in0, scalar1=s1, scalar2=None, op0=op0),
                     reads=reads, writes=writes)
            else:
                P.op(eng, lambda e: e.tensor_scalar(out=out, in0=in0, scalar1=s1, scalar2=s2, op0=op0, op1=op1),
                     reads=reads, writes=writes)

        def tt_op(eng, out, in0, in1, op, reads, writes):
            P.op(eng, lambda e: e.tensor_tensor(out=out, in0=in0, in1=in1, op=op), reads=reads, writes=writes)

        def copy_op(eng, out, in_, reads, writes):
            if eng is act:
                act_op(out, in_, AF.Copy, reads, writes)
            else:
                P.op(eng, lambda e: e.tensor_copy(out=out, in_=in_), reads=reads, writes=writes)

        evc = {"i": 0}

        def evac_engine():
            evc["i"] += 1
            return act if evc["i"] % 2 == 0 else dve

        scr_groups = {}
        cast_q = {}

        def make_piece(name, group, srcs):
            tot = sum(a * b for _, a, b in srcs)
            assert tot <= SLOT_ELEMS, (name, tot)
            t = nc.dram_tensor("scr_" + name, [128, tot], BF16, kind="Internal").ap()
            if group not in scr_groups:
                scr_groups[group] = Buf("scr_" + group)
            gb = scr_groups[group]
            off = 0
            for src, a, b in srcs:
                cast_q.setdefault(group, []).append(
                    (t[:, off:off + a * b].rearrange("p (a b) -> p a b", a=a), src, gb))
                off += a * b
            return (t, tot, gb)

        def flush_casts(groups):
            for g in groups:
                for dst, src, gb in cast_q.pop(g, []):
                    P.dma(pool, dst, src, writes=[gb], owner=gb)

        def emit_casts(n):
            for g in ("inB", "gate", "merge", "out", "gu", "down"):
                while n > 0 and cast_q.get(g):
                    dst, src, gb = cast_q[g].pop(0)
                    P.dma(pool, dst, src, writes=[gb], owner=gb)
                    n -= 1

        def kview(w, c0, c1):
            return w[:, c0:c1].rearrange("(k p) c -> p k c", p=128)

        P.op(pool, lambda e: e.memset(ones32[:], 1.0), writes=[ON_B])
        P.op(pool, lambda e: e.affine_select(out=ident[:], in_=ones32[:], pattern=[[-1, 128]],
                                             compare_op=ALU.is_equal, fill=0.0, base=0, channel_multiplier=1),
             reads=[ON_B], writes=[ID_B])
        ts_op(dve, selL[:], ones32[:], -1.0, None, ALU.mult, None, [ON_B], [SELL_B])
        P.op(pool, lambda e: e.affine_select(out=triIN[:], in_=selL[:], pattern=[[1, 128]],
                                             compare_op=ALU.is_ge, fill=0.0, base=0, channel_multiplier=-1),
             reads=[SELL_B], writes=[TRIIN_B])
        copy_op(dve, tmpb[:], selL[:], [SELL_B], [TMPB_B])
        P.op(pool, lambda e: e.affine_select(out=triN[:], in_=tmpb[:], pattern=[[-1, 128]],
                                             compare_op=ALU.is_ge, fill=0.0, base=0, channel_multiplier=1),
             reads=[TMPB_B], writes=[TRIN_B])
        tt_op(dve, cmpN[:], tmpb[:], triN[:], ALU.subtract, [TMPB_B, TRIN_B], [CMPN_B])
        P.op(pool, lambda e: e.affine_select(out=selL[:], in_=ones32[:], pattern=[[0, 128]],
                                             compare_op=ALU.is_ge, fill=0.0, base=-127, channel_multiplier=1),
             reads=[ON_B, TRIIN_B, TMPB_B], writes=[SELL_B])
        P.op(pool, lambda e: e.memset(selS[:], 0.0), writes=[SELS_B])
        P.op(pool, lambda e: e.memset(selS[0:1, :], -1.0), writes=[SELS_B])
        P.op(pool, lambda e: e.memset(selS[64:65, :], -1.0), writes=[SELS_B])
        P.op(pool, lambda e: e.memset(AR[:, 20:24, :], 0.0), writes=AR_B[20:24])
        Vv = V[:].rearrange("p k (q c) -> p k q c", q=4)
        P.op(pool, lambda e: e.memset(Vv[:, :, :, 64:128], 1.0), writes=V_B)

        pieces = {}
        for nm, c0, n in (("qsb", 0, 2), ("ksb", 512, 2), ("vsb", 1024, 2),
                          ("qfx", 1536, 2), ("kfx", 2048, 2), ("vfx", 2560, 2), ("gate", 3080, 8)):
            grp = "inA" if nm.endswith("sb") else ("inB" if nm != "gate" else "gate")
            pieces[nm] = [make_piece(f"{nm}{i}", grp, [(kview(w_in_d, c0 + 256 * i, c0 + 256 * (i + 1)), 8, 256)])
                          for i in range(n)]
        pieces["merge"] = [make_piece(f"mg{i}", "merge",
                                      [(w_bsb_d[:, 256 * i:256 * (i + 1)].rearrange("(k p) c -> p k c", p=128), 4, 256),
                                       (w_bfx_d[:, 256 * i:256 * (i + 1)].rearrange("(k p) c -> p k c", p=128), 4, 256)])
                           for i in range(4)]
        pieces["out"] = [make_piece(f"wo{i}", "out", [(kview(w_out_d, 256 * i, 256 * (i + 1)), 8, 256)])
                         for i in range(4)]
        pieces["gu"] = [make_piece(f"gu{j}", "gu",
                                   [(kview(w_fg_d, 128 * j, 128 * (j + 1)), 8, 128),
                                    (kview(w_fu_d, 128 * j, 128 * (j + 1)), 8, 128)])
                        for j in range(NJ)]
        JG = [(0, 4), (4, 8), (8, 12), (12, 16), (16, 20), (20, 22)]
        pieces["down"] = [[make_piece(f"dn{h}_{g}", "down",
                                      [(w_fd_d[j0 * 128:j1 * 128, 512 * h:512 * (h + 1)].rearrange("(j p) c -> p j c", p=128),
                                        j1 - j0, 512)])
                           for g, (j0, j1) in enumerate(JG)] for h in range(2)]

        flush_casts(["inA"])

        def small_load(t, tb, src):
            P.dma(sp, t, src, writes=[tb])

        small_load(cT[:], CT_B, cT_d)
        small_load(b_adaT[:], BADA_B, b_adaT_d)
        small_load(g_mixT[:], GMIX_B, g_mixT_d)
        small_load(g_ffnT[:], GFFN_B, g_ffnT_d)
        small_load(gfin[:], GFIN_B, gfin_d)
        small_load(bfor[:], BFOR_B, bfor_d)
        small_load(bgateT[:], BGATE_B, bgateT_d)
        P.dma(pool, wf[:], kview(w_in_d, 3072, 3080), writes=[WF_B])

        act_op(csig[:], cT[:], AF.Sigmoid, [CT_B], [CSIG_B])
        tt_op(dve, cact[:], cT[:], csig[:], ALU.mult, [CT_B, CSIG_B], [CACT_B])
        for pc in range(24):
            hb = pc % 2
            stg = X[:, 2 * hb:2 * hb + 2, :].rearrange("p a b -> p (a b)").rearrange("p (k c) -> p k c", k=8)
            stgb = [X_B[2 * hb], X_B[2 * hb + 1]]
            P.dma(sp, stg, w_ada_d[:, 256 * pc:256 * (pc + 1)].rearrange("(k p) c -> p k c", p=128),
                  writes=stgb, owner=stgb[0])
            bank, bankb = next_bank()
            for ct in range(2):
                for kc in range(8):
                    mm(bank[:, 2 * ct:2 * ct + 2], stg[:, kc, 128 * ct:128 * (ct + 1)], cact[:, kc, :],
                       kc == 0, kc == 7, stgb + [CACT_B], [bankb], skip=True)
            copy_op(dve, modraw[:, 2 * pc:2 * (pc + 1), :],
                    bank[:, 0:4].rearrange("p (a b) -> p a b", a=2), [bankb], [MODRAW_B])
        for b in range(NSEQ):
            tt_op(dve, modT[:, :, b], modraw[:, :, b], b_adaT[:], ALU.add, [MODRAW_B, BADA_B], [MODT_B])
        for b in range(NSEQ):
            for which, (v, gsrc, gbuf) in enumerate(((1, g_mixT, GMIX_B), (4, g_ffnT, GFFN_B))):
                P.op(dve, lambda e, b=b, which=which, v=v, gsrc=gsrc: e.scalar_tensor_tensor(
                    out=geff[:, b, which, :], in0=modT[:, 8 * v:8 * v + 8, b], scalar=1.0, in1=gsrc[:],
                    op0=ALU.add, op1=ALU.mult), reads=[MODT_B, gbuf], writes=[GEFF_B])

        def build_gate_bc(b):
            k = 0
            for gi, v in ((0, 2), (1, 5)):
                for half in range(2):
                    bank, bankb = next_bank()
                    for q in range(4):
                        ch = half * 4 + q
                        s = k % 2
                        k += 1
                        ts_op(dve, Gm[:, s, :], ones32[:], modT[:, 8 * v + ch, b:b + 1], None, ALU.mult, None,
                              [ON_B, MODT_B], [GM_B[s]])
                        mm(bank[:, 128 * q:128 * (q + 1)], Gm[:, s, :], ident[:], True, True,
                           [GM_B[s], ID_B], [bankb], skip=True)
                    copy_op(evac_engine(), GBC[gi][:, 512 * half:512 * (half + 1)], bank[:], [bankb], [GBC_B[gi][half]])

        junk = AR[:, 22:24, :]
        JUNK_B = [AR_B[22], AR_B[23]]

        def sumsq_rstd():
            for i in range(4):
                act_op(junk, X[:, i, :].rearrange("p (a b) -> p a b", a=2), AF.Square,
                       [X_B[i]], JUNK_B + [SS_B], accum_out=ss[:, i:i + 1])
            act_op(lnv[:], ss[:], AF.Ln, [SS_B], [LNV_B], scale=1.0 / D, bias=EPS)
            act_op(rstd[:], lnv[:], AF.Exp, [LNV_B], [RSTD_B], scale=-0.5)

        def norm_to_hT(b, which, vshift):
            sumsq_rstd()
            for i in range(4):
                ts_op(dve, Dm[:, i, :], ident[:], rstd[:, i:i + 1], None, ALU.mult, None, [ID_B, RSTD_B], [DM_B[i]])
            for kc in range(8):
                bank, bankb = next_bank()
                for i in range(4):
                    mm(bank[:, 128 * i:128 * (i + 1)], X[:, i, 128 * kc:128 * (kc + 1)], Dm[:, i, :], True, True,
                       [X_B[i], DM_B[i]], [bankb], skip=True)
                eng = evac_engine()
                sc = geff[:, b, which, kc:kc + 1]
                bi = modT[:, 8 * vshift + kc, b:b + 1]
                if eng is act:
                    act_op(HT[:, kc, :], bank[:], AF.Identity, [bankb, GEFF_B, MODT_B], [HT_B[kc]], scale=sc, bias=bi)
                else:
                    ts_op(dve, HT[:, kc, :], bank[:], sc, bi, ALU.mult, ALU.add, [bankb, GEFF_B, MODT_B], [HT_B[kc]])

        def load_x(b, c):
            for i in range(4):
                r0 = c * CH + i * 128
                P.dma(sp, X[:, i, :], x_d[b, r0:r0 + 128, :], writes=[X_B[i]])

        stream = {"n": 0}

        def run_steps(steps):
            pidx = [i for i, s_ in enumerate(steps) if s_[0] is not None]
            slot_of = {}
            ptr = 0
            for i, (piece, fn) in enumerate(steps):
                if piece is not None:
                    k = pidx.index(i)
                    while ptr < len(pidx) and ptr <= k + LOOKAHEAD:
                        t, tot, gb = steps[pidx[ptr]][0]
                        sl = stream["n"] % NSLOT
                        stream["n"] += 1
                        P.dma(sp, WS[sl][:, 0:tot], t[:, :], reads=[gb], writes=[WS_B[sl]])
                        slot_of[pidx[ptr]] = sl
                        ptr += 1
                    sl = slot_of[i]
                    fn(WS[sl], WS_B[sl])
                else:
                    fn(None, None)

        QT0 = 16

        def proj_fm(piece_w, piece_b, ncols_tiles, consume):
            wv = piece_w[:, 0:2048].rearrange("p (k c) -> p k c", k=8)
            for ct in range(ncols_tiles):
                bank, bankb = next_bank()
                for kc in range(8):
                    mm(bank[:], wv[:, kc, 128 * ct:128 * (ct + 1)], HT[:, kc, :], kc == 0, kc == 7,
                       [piece_b, HT_B[kc]], [bankb])
                consume(ct, bank, bankb)

        def q_steps(nm):
            steps = []
            for i in range(2):
                def fn(w, wb, i=i):
                    def consume(ct, bank, bankb):
                        pr = 2 * i + ct
                        eng = evac_engine()
                        if eng is act:
                            act_op(AR[:, QT0 + pr, :], bank[:], AF.Copy, [bankb], [AR_B[QT0 + pr]], scale=0.125)
                        else:
                            ts_op(dve, AR[:, QT0 + pr, :], bank[:], 0.125, None, ALU.mult, None, [bankb], [AR_B[QT0 + pr]])
                    proj_fm(w, wb, 2, consume)
                steps.append((pieces[nm][i], fn))
            return steps

        def k_steps(nm, c):
            steps = []
            for i in range(2):
                def fn(w, wb, i=i):
                    def consume(ct, bank, bankb):
                        pr = 2 * i + ct
                        copy_op(evac_engine(), KT[:, pr, c * CH:(c + 1) * CH], bank[:], [bankb], [KT_B[pr][c]])
                    proj_fm(w, wb, 2, consume)
                steps.append((pieces[nm][i], fn))
            return steps

        def v_steps(nm, c):
            steps = []
            for g in range(2):
                def fn(w, wb, g=g):
                    wv = w[:, 0:2048].rearrange("p (k c) -> p k c", k=8)
                    for i in range(4):
                        blk = 4 * c + i
                        bank, bankb = next_bank()
                        for kc in range(8):
                            mm(bank[:, 0:256], HT[:, kc, 128 * i:128 * (i + 1)], wv[:, kc, :], kc == 0, kc == 7,
                               [wb, HT_B[kc]], [bankb])
                        bv = bank[:, 0:256].rearrange("p (q t d) -> p q t d", q=2, t=2)
                        copy_op(act, Vv[:, blk, 2 * g:2 * g + 2, 0:64], bv[:, :, 0, :], [bankb], [V_B[blk]])
                        copy_op(dve, Vv[:, blk, 2 * g:2 * g + 2, 128:192], bv[:, :, 1, :], [bankb], [V_B[blk]])
                steps.append((pieces[nm][g], fn))
            return steps

        def make_bg(b, c, qt0):
            plist = pieces["qsb"] + pieces["ksb"] + pieces["vsb"]
            slots = {}
            st_ = {"ptr": 0}

            def ensure(n):
                while st_["ptr"] < len(plist) and st_["ptr"] <= n + LOOKAHEAD:
                    t, tot, gb = plist[st_["ptr"]]
                    sl = stream["n"] % NSLOT
                    stream["n"] += 1
                    P.dma(sp, WS[sl][:, 0:tot], t[:, :], reads=[gb], writes=[WS_B[sl]])
                    slots[st_["ptr"]] = sl
                    st_["ptr"] += 1
                return WS[slots[n]], WS_B[slots[n]]

            def pre():
                sumsq_rstd()
                for i in range(4):
                    ts_op(dve, Dm[:, i, :], ident[:], rstd[:, i:i + 1], None, ALU.mult, None, [ID_B, RSTD_B], [DM_B[i]])

            groups = []
            for kc in range(8):
                for hf in range(2):
                    def pe_part(BG, BGb, kc=kc, hf=hf):
                        for ii in range(2):
                            i = 2 * hf + ii
                            mm(BG[:, 128 * ii:128 * (ii + 1)], X[:, i, 128 * kc:128 * (kc + 1)], Dm[:, i, :], True, True,
                               [X_B[i], DM_B[i]], [BGb], skip=True)

                    def ev_part(BG, BGb, eng, kc=kc, hf=hf):
                        sc, bi = geff[:, b, 0, kc:kc + 1], modT[:, kc, b:b + 1]
                        o = HT[:, kc, 256 * hf:256 * (hf + 1)]
                        if eng is act:
                            act_op(o, BG[:, 0:256], AF.Identity, [BGb, GEFF_B, MODT_B], [HT_B[kc]], scale=sc, bias=bi)
                        else:
                            ts_op(dve, o, BG[:, 0:256], sc, bi, ALU.mult, ALU.add, [BGb, GEFF_B, MODT_B], [HT_B[kc]])
                    groups.append((pe_part, ev_part))
            for wi, nm in enumerate(("qsb", "ksb")):
                for i in range(2):
                    for ct in range(2):
                        for hf in range(2):
                            pr = 2 * i + ct

                            def pe_part(BG, BGb, n=2 * wi + i, ct=ct, hf=hf):
                                w, wb = ensure(n)
                                wv = w[:, 0:2048].rearrange("p (k c) -> p k c", k=8)
                                for kc in range(8):
                                    mm(BG[:, 0:256], wv[:, kc, 128 * ct:128 * (ct + 1)], HT[:, kc, 256 * hf:256 * (hf + 1)],
                                       kc == 0, kc == 7, [wb, HT_B[kc]], [BGb], skip=True)

                            if nm == "qsb":
                                def ev_part(BG, BGb, eng, pr=pr, hf=hf):
                                    o = AR[:, qt0 + pr, 256 * hf:256 * (hf + 1)]
                                    if eng is act:
                                        act_op(o, BG[:, 0:256], AF.Copy, [BGb], [AR_B[qt0 + pr]], scale=0.125)
                                    else:
                                        ts_op(dve, o, BG[:, 0:256], 0.125, None, ALU.mult, None, [BGb], [AR_B[qt0 + pr]])
                            else:
                                def ev_part(BG, BGb, eng, pr=pr, hf=hf):
                                    copy_op(eng, KT[:, pr, c * CH + 256 * hf:c * CH + 256 * (hf + 1)], BG[:, 0:256],
                                            [BGb], [KT_B[pr][c]])
                            groups.append((pe_part, ev_part))
            for g in range(2):
                for i in range(4):
                    def pe_part(BG, BGb, g=g, i=i):
                        w, wb = ensure(4 + g)
                        wv = w[:, 0:2048].rearrange("p (k c) -> p k c", k=8)
                        for kc in range(8):
                            mm(BG[:, 0:256], HT[:, kc, 128 * i:128 * (i + 1)], wv[:, kc, :], kc == 0, kc == 7,
                               [wb, HT_B[kc]], [BGb], skip=True)

                    def ev_part(BG, BGb, eng, g=g, i=i):
                        blk = 4 * c + i
                        bv = BG[:, 0:256].rearrange("p (q t d) -> p q t d", q=2, t=2)
                        copy_op(eng, Vv[:, blk, 2 * g:2 * g + 2, 0:64], bv[:, :, 0, :], [BGb], [V_B[blk]])
                        copy_op(dve, Vv[:, blk, 2 * g:2 * g + 2, 128:192], bv[:, :, 1, :], [BGb], [V_B[blk]])
                    groups.append((pe_part, ev_part))
            return dict(load=lambda: load_x(b, c), pre=pre, groups=groups)

        def v3(t, col0):
            return t[:, :].rearrange("p (s c) -> p s c", s=2)[:, :, col0:CH]

        def fill(bank_i, n, first=False):
            for j in range(n):
                P.op(pe, lambda e: e.matmul(PB[bank_i][:, 384:512], lhsT=triN[:], rhs=triN[:], start=True, stop=True,
                                            skip_group_check=True),
                     reads=[TRIN_B] if (first and j == 0) else [], writes=[PB_B[bank_i]] if (first and j == 0) else [])

        def sb_attention(c, qt0, bg):
            nst = 4 * c + 4
            nf = NF_SB if bg is None else 1
            Zp, Zb = PP[1], [PB_B[2], PB_B[3]]
            Cp, Cb = PP[2], [PB_B[4], PB_B[5]]
            XPp, XPb = GBC[1], GBC_B[1]
            for p in range(4):
                Y, Yb = PB[6], PB_B[6]
                qb = AR_B[qt0 + p]

                def kbof(k):
                    return nst - 1 - k

                def col0of(k):
                    kb = kbof(k)
                    return 128 * (kb - 4 * c) if kb >= 4 * c else 0

                def pe1(k):
                    kb, c0 = kbof(k), col0of(k)
                    for s in range(2):
                        mm(Zp[:, CH * s + c0:CH * (s + 1)], KT[64 * s:64 * s + 64, p, 128 * kb:128 * (kb + 1)],
                           AR[64 * s:64 * s + 64, qt0 + p, c0:CH], True, True,
                           [KT_B[p][kb // 4], qb], [Zb[s]])

                def act1(k):
                    c0 = col0of(k)
                    e, eb = EE3[k % 3], EE3_B[k % 3]
                    act_op(v3(e, c0), v3(Zp, c0), AF.Exp, Zb, eb)
                    if kbof(k) >= 4 * c:
                        n = CH - c0
                        P.op(pool, lambda en, e=e, c0=c0, n=n: en.affine_select(
                            out=v3(e, c0), in_=v3(e, c0), pattern=[[0, 2], [1, n]], compare_op=ALU.is_gt, fill=0.0,
                            base=0, channel_multiplier=-1), reads=eb, writes=eb)

                def act2(k):
                    c0 = col0of(k)
                    act_op(v3(SPP[k % 2], c0), v3(EE3[k % 3], c0), AF.Ln, EE3_B[k % 3], SPP_B[k % 2], bias=1.0)

                def pe2(k):
                    c0 = col0of(k)
                    for s in range(2):
                        mm(Cp[:, CH * s + c0:CH * (s + 1)], triN[:], SPP[k % 2][:, CH * s + c0:CH * (s + 1)], k == 0, True,
                           [TRIN_B, SPP_B[k % 2][s]], [Cb[s]], skip=True)

                def act3(k):
                    c0 = col0of(k)
                    act_op(v3(XPp, c0), v3(Cp, c0), AF.Exp, Cb, XPb)

                def pe3(k):
                    c0 = col0of(k)
                    for s in range(2):
                        mm(Cp[:, CH * s + c0:CH * (s + 1)], cmpN[:], SPP[k % 2][:, CH * s + c0:CH * (s + 1)], False, True,
                           [CMPN_B, SPP_B[k % 2][s]], [Cb[s]], skip=True)

                def dve_w(k):
                    c0 = col0of(k)
                    for s in range(2):
                        sl = slice(CH * s + c0, CH * (s + 1))
                        tt_op(dve, WW[:, sl], EE3[k % 3][:, sl], XPp[:, sl], ALU.mult,
                              [EE3_B[k % 3][s], XPb[s]], [WW_B[s]])

                def pe4(k):
                    kb, c0 = kbof(k), col0of(k)
                    for s in range(2):
                        vc = 192 * p + 128 * s
                        mm(Y[64 * s:64 * s + 64, c0:CH], V[:, kb, vc:vc + 64], WW[:, CH * s + c0:CH * (s + 1)],
                           k == 0, k == nst - 1, [V_B[kb], WW_B[s]], [Yb], skip=True)

                fill(7, 1, first=True)
                if bg is not None and p == 0:
                    bg["load"]()
                pe1(0)
                act1(0)
                pe1(1)
                act2(0)
                for k in range(nst):
                    if bg is not None and (bg["groups"] or not bg.get("drained")):
                        nf = 0
                        if not bg["groups"]:
                            bg["drained"] = True
                            fill(7, 1, first=True)
                    else:
                        nf = NF_SB
                    pe2(k)
                    if k > 0:
                        pe4(k - 1)
                    fill(7, nf)
                    if k + 1 < nst:
                        act1(k + 1)
                    if k + 2 < nst:
                        pe1(k + 2)
                        fill(7, nf)
                    act3(k)
                    if k < nst - 1:
                        pe3(k)
                        fill(7, nf)
                    grp = None
                    if bg is not None:
                        if p == 0 and k == 2:
                            bg["pre"]()
                        elif (p > 0 or k > 2) and bg["groups"]:
                            grp = bg["groups"].pop(0)
                            grp[0](PB[7], PB_B[7])
                    dve_w(k)
                    if grp is not None:
                        grp[1](PB[7], PB_B[7], dve)
                    if k + 1 < nst:
                        act2(k + 1)
                    if k == 4 and c >= 3:
                        emit_casts(5)
                pe4(nst - 1)
                copy_op(dve, YSB[:, p, c * CH:(c + 1) * CH], Y[:], [Yb], [YSB_B[p][c]])
            if bg is not None:
                while bg["groups"]:
                    grp = bg["groups"].pop(0)
                    bank, bankb = next_bank()
                    grp[0](bank, bankb)
                    grp[1](bank, bankb, evac_engine())

        def fox_prep_thunks(c):
            st8 = {}

            def prepA(i):
                def fn(w_, wb_):
                    bank, bankb = next_bank()
                    for kc in range(8):
                        mm(bank[:, 0:8], HT[:, kc, 128 * i:128 * (i + 1)], wf[:, kc, :], kc == 0, kc == 7,
                           [HT_B[kc], WF_B], [bankb])
                    tt_op(dve, ft[:], bank[:, 0:8], bfor[:], ALU.add, [bankb, BFOR_B], [FT_B])
                    act_op(fe[:], ft[:], AF.Exp, [FT_B], [FE_B], scale=-1.0)
                    act_op(fl[:, i, :], fe[:], AF.Ln, [FE_B], [FL_B[i]], bias=1.0)
                return fn

            def prepB(i):
                def fn(w_, wb_):
                    blk = 4 * c + i
                    bank2, bank2b = next_bank()
                    mm(bank2[:, 0:8], triIN[:], fl[:, i, :], True, blk == 0, [TRIIN_B, FL_B[i]], [bank2b], skip=True)
                    if blk > 0:
                        mm(bank2[:, 0:8], selL[:], ckT[:, blk - 1, :], False, True, [SELL_B, CK_B[blk - 1]], [bank2b], skip=True)
                    copy_op(dve, ckT[:, blk, :], bank2[:, 0:8], [bank2b], [CK_B[blk]])
                return fn

            def biasA(w_, wb_):
                bank3, bank3b = next_bank()
                mm(bank3[:, 0:8], selL[:], ckT[:, 4 * c + 1, :], True, True, [SELL_B, CK_B[4 * c + 1]], [bank3b], skip=True)
                copy_op(dve, crefbc[:], bank3[:, 0:8], [bank3b], [CREF_B])

            def biasB(w_, wb_):
                nkb = 4 * c + 4
                for hh in range(8):
                    ts_op(dve, biasT[:, hh, 0:nkb], ckT[:, 0:nkb, hh], crefbc[:, hh:hh + 1], -1.0, ALU.subtract, ALU.mult,
                          CK_B[0:nkb] + [CREF_B], [BIAS_B])

            def shiftq(w_, wb_):
                for hh in range(8):
                    bank4, bank4b = next_bank()
                    for i in range(4):
                        mm(bank4[0:1, 128 * i:128 * (i + 1)], ckT[:, 4 * c + i, hh:hh + 1], ident[:], True, True,
                           [CK_B[4 * c + i], ID_B], [bank4b], skip=True)
                    r0 = 0 if hh % 2 == 0 else 64
                    ts_op(dve, AR[r0:r0 + 1, 20 + hh // 2, :], bank4[0:1, :], -1.0, crefbc[0:1, hh:hh + 1], ALU.mult, ALU.add,
                          [bank4b, CREF_B], [AR_B[20 + hh // 2]])

            def seq(*fns):
                def fn(w_, wb_):
                    for f_ in fns:
                        f_(w_, wb_)
                return fn

            return [seq(prepA(0), prepA(1)), seq(prepA(2), prepA(3), prepB(0)), seq(prepB(1)), seq(prepB(2)),
                    seq(prepB(3)), seq(biasA), seq(biasB), seq(shiftq)]

        def fox_attention(c):
            nst = 4 * c + 4
            for p in range(4):
                ybanks = (6, 7) if p % 2 == 0 else (4, 5)
                heads = []
                for s in range(2):
                    r0 = 64 * s
                    heads.append(dict(
                        s=s, r0=r0, hh=2 * p + s,
                        qb=AR_B[QT0 + p],
                        vc=192 * p + 64 * s,
                        Z=[PB[2 + s], PB[s]], Zb=[PB_B[2 + s], PB_B[s]],
                        Y=PB[ybanks[s]], Yb=PB_B[ybanks[s]],
                        pb=[(SPP[0][:, CH * s:CH * (s + 1)], SPP_B[0][s]), (SPP[1][:, CH * s:CH * (s + 1)], SPP_B[1][s])]))

                def col0of(kb):
                    return 128 * (kb - 4 * c) if kb >= 4 * c else 0

                def pe1(h, kb):
                    c0 = col0of(kb)
                    r0 = h["r0"]
                    mm(h["Z"][kb % 2][:, c0:CH], KT[r0:r0 + 64, p, 128 * kb:128 * (kb + 1)], AR[r0:r0 + 64, QT0 + p, c0:CH],
                       True, True, [KT_B[p][kb // 4], h["qb"]], [h["Zb"][kb % 2]])

                def pe1s(h, kb):
                    c0 = col0of(kb)
                    r0 = h["r0"]
                    mm(h["Z"][kb % 2][:, c0:CH], selS[r0:r0 + 64, :], AR[r0:r0 + 64, 20 + p, c0:CH],
                       False, True, [SELS_B, AR_B[20 + p]], [h["Zb"][kb % 2]], skip=True)

                for kb0 in range(2):
                    for h in heads:
                        pe1(h, kb0)
                    for h in heads:
                        pe1s(h, kb0)
                for kb in range(nst):
                    first, last = kb == 0, kb == nst - 1
                    c0 = col0of(kb)
                    n = CH - c0
                    for h in heads:
                        pt, ptb = h["pb"][kb % 2]
                        act_op(pt[:, c0:CH], h["Z"][kb % 2][:, c0:CH], AF.Exp, [h["Zb"][kb % 2], BIAS_B], [ptb],
                               bias=biasT[:, h["hh"], kb:kb + 1])
                        if kb >= 4 * c:
                            P.op(pool, lambda e, pt=pt, c0=c0, n=n: e.affine_select(
                                out=pt[:, c0:CH], in_=pt[:, c0:CH], pattern=[[1, n]], compare_op=ALU.is_ge, fill=0.0,
                                base=0, channel_multiplier=-1), reads=[ptb], writes=[ptb])
                    if kb + 2 < nst:
                        for h in heads:
                            pe1(h, kb + 2)
                        for h in heads:
                            pe1s(h, kb + 2)
                        fill(4, NF_FX)
                    for h in heads:
                        pt, ptb = h["pb"][kb % 2]
                        mm(h["Y"][:, c0:CH], V[:, kb, h["vc"]:h["vc"] + 128], pt[:, c0:CH], first, last,
                           [V_B[kb], ptb], [h["Yb"]], skip=True)
                    fill(4, NF_FX)
                for h in heads:
                    s = h["s"]
                    yr = slice(0, 64) if s == 0 else slice(64, 128)
                    dr = slice(64, 128) if s == 0 else slice(0, 64)
                    P.op(dve, lambda e, h=h, s=s, dr=dr: e.reciprocal(out=XC[s][dr, :], in_=h["Y"][dr, :]),
                         reads=[h["Yb"]], writes=[XC_B[s]])
                    tt_op(dve, HT[yr, p, :], h["Y"][yr, :], XC[s][dr, :], ALU.mult, [h["Yb"], XC_B[s]], [HT_B[p]])

        def gate_steps():
            steps = []
            for i in range(8):
                def fn(w, wb, i=i):
                    def consume(ct, bank, bankb):
                        j = 2 * i + ct
                        act_op(AR[:, j, :], bank[:], AF.Sigmoid, [bankb, BGATE_B], [AR_B[j]], bias=bgateT[:, j:j + 1])
                    proj_fm(w, wb, 2, consume)
                steps.append((pieces["gate"][i], fn))
            return steps

        def merge_steps(c):
            steps = []
            for i in range(4):
                def fn(w, wb, i=i):
                    wv = w[:, 0:2048].rearrange("p (k c) -> p k c", k=8)
                    for ct in range(2):
                        j = 2 * i + ct
                        b1, b1b = next_bank()
                        for kc in range(4):
                            mm(b1[:], wv[:, kc, 128 * ct:128 * (ct + 1)], YSB[:, kc, c * CH:(c + 1) * CH], kc == 0, kc == 3,
                               [wb, YSB_B[kc][c]], [b1b])
                        b2, b2b = next_bank()
                        for kc in range(4):
                            mm(b2[:], wv[:, 4 + kc, 128 * ct:128 * (ct + 1)], HT[:, kc, :], kc == 0, kc == 3,
                               [wb, HT_B[kc]], [b2b])
                        t1, t1b = (E_[ct], E_B[ct])
                        t2, t2b = (XC[ct], XC_B[ct])
                        tt_op(dve, t1[:], AR[:, j, :], b1[:], ALU.mult, [AR_B[j], b1b], [t1b])
                        tt_op(dve, t2[:], AR[:, 8 + j, :], b2[:], ALU.mult, [AR_B[8 + j], b2b], [t2b])
                        tt_op(pool if ct == 0 else dve, AR[:, 16 + j, :], t1[:], t2[:], ALU.add, [t1b, t2b], [AR_B[16 + j]])
                steps.append((pieces["merge"][i], fn))
            return steps

        ALT_T = [(SPP[0][:, :].bitcast(F32), SPP_B[0]), (SPP[1][:, :].bitcast(F32), SPP_B[1]),
                 (WW[:, :].bitcast(F32), WW_B)]

        def resid_update(i, cols, bank, bankb, gi, half_b, k, alt=False):
            if alt:
                t, tbs = ALT_T[k % 3]
            else:
                t, tb = (E_[k % 2], E_B[k % 2]) if (k // 2) % 2 == 0 else (XC[k % 2], XC_B[k % 2])
                tbs = [tb]
            n = cols.stop - cols.start
            tt_op(dve, t[:, 0:n], bank[:, 0:n], GBC[gi][:, cols], ALU.mult, [bankb, half_b], tbs)
            tt_op(pool if k % 2 == 0 else dve, X[:, i, cols], X[:, i, cols], t[:, 0:n], ALU.add, [X_B[i]] + tbs, [X_B[i]])

        def out_steps():
            steps = []
            for q in range(4):
                def fn(w, wb, q=q):
                    wv = w[:, 0:2048].rearrange("p (k c) -> p k c", k=8)
                    for i in range(4):
                        bank, bankb = next_bank()
                        for kc in range(8):
                            mm(bank[:, 0:256], AR[:, 16 + kc, 128 * i:128 * (i + 1)], wv[:, kc, :], kc == 0, kc == 7,
                               [wb, AR_B[16 + kc]], [bankb])
                        resid_update(i, slice(256 * q, 256 * (q + 1)), bank, bankb, 0, GBC_B[0][q // 2], i)
                steps.append((pieces["out"][q], fn))
            return steps

        def gu_steps():
            steps = []
            for j in range(NJ):
                def fn(w, wb, j=j):
                    wv = w[:, 0:2048].rearrange("p (t k c) -> p t k c", t=2, k=8)
                    bg, bgb = next_bank()
                    for kc in range(8):
                        mm(bg[:], wv[:, 0, kc, :], HT[:, kc, :], kc == 0, kc == 7, [wb, HT_B[kc]], [bgb])
                    bu, bub = next_bank()
                    for kc in range(8):
                        mm(bu[:], wv[:, 1, kc, :], HT[:, kc, :], kc == 0, kc == 7, [wb, HT_B[kc]], [bub])
                    t, tb = (E_[j % 2], E_B[j % 2])
                    act_op(t[:], bg[:], AF.Silu, [bgb], [tb])
                    tt_op(dve, AR[:, j, :], t[:], bu[:], ALU.mult, [tb, bub], [AR_B[j]])
                steps.append((pieces["gu"][j], fn))
            return steps

        def make_pnorm(b, cn):
            def load(wv):
                def fn():
                    for ii in range(2):
                        r0 = cn * CH + (2 * wv + ii) * 128
                        P.dma(sp, EE[ii][:, :], x_d[b, r0:r0 + 128, :], writes=EE_B[ii], owner=EE_B[ii][0])
                return fn

            def partA(wv):
                def fn():
                    for ii in range(2):
                        i = 2 * wv + ii
                        act_op(junk, EE[ii][:, :].rearrange("p (a b) -> p a b", a=2), AF.Square,
                               EE_B[ii], JUNK_B + [SS_B], accum_out=ss[:, i:i + 1])
                    act_op(lnv[:, 2 * wv:2 * wv + 2], ss[:, 2 * wv:2 * wv + 2], AF.Ln, [SS_B], [LNV_B], scale=1.0 / D, bias=EPS)
                    act_op(rstd[:, 2 * wv:2 * wv + 2], lnv[:, 2 * wv:2 * wv + 2], AF.Exp, [LNV_B], [RSTD_B], scale=-0.5)
                    for ii in range(2):
                        i = 2 * wv + ii
                        ts_op(dve, Dm[:, i, :], ident[:], rstd[:, i:i + 1], None, ALU.mult, None, [ID_B, RSTD_B], [DM_B[i]])
                return fn

            def partB(wv):
                def fn():
                    for kc in range(8):
                        bank, bankb = next_bank()
                        for ii in range(2):
                            i = 2 * wv + ii
                            mm(bank[:, 128 * ii:128 * (ii + 1)], EE[ii][:, 128 * kc:128 * (kc + 1)], Dm[:, i, :], True, True,
                               EE_B[ii] + [DM_B[i]], [bankb], skip=True)
                        eng = evac_engine()
                        sc = geff[:, b, 0, kc:kc + 1]
                        bi = modT[:, kc, b:b + 1]
                        o = HT[:, kc, 256 * wv:256 * (wv + 1)]
                        if eng is act:
                            act_op(o, bank[:, 0:256], AF.Identity, [bankb, GEFF_B, MODT_B], [HT_B[kc]], scale=sc, bias=bi)
                        else:
                            ts_op(dve, o, bank[:, 0:256], sc, bi, ALU.mult, ALU.add, [bankb, GEFF_B, MODT_B], [HT_B[kc]])
                return fn

            return {(0, 0): [load(0)], (0, 1): [partA(0)], (0, 3): [partB(0), load(1)], (0, 5): [partA(1)],
                    (1, 1): [partB(1)]}

        def down_steps(hooks):
            steps = []
            ACC = [(PB[4], PB_B[4]), (PB[5], PB_B[5]), (PB[6], PB_B[6]), (PB[7], PB_B[7])]
            for hf in range(2):
                for g, (j0, j1) in enumerate(JG):
                    def fn(w, wb, hf=hf, g=g, j0=j0, j1=j1):
                        for hk in hooks.get((hf, g), []):
                            hk()
                        wv = w[:, 0:(j1 - j0) * 512].rearrange("p (j c) -> p j c", j=j1 - j0)
                        for i in range(4):
                            for j in range(j0, j1):
                                mm(ACC[i][0][:], AR[:, j, 128 * i:128 * (i + 1)], wv[:, j - j0, :], j == 0, j == NJ - 1,
                                   [wb, AR_B[j]], [ACC[i][1]])
                        if j1 == NJ:
                            for i in range(4):
                                resid_update(i, slice(512 * hf, 512 * (hf + 1)), ACC[i][0], ACC[i][1], 1, GBC_B[1][hf], i,
                                             alt=True)
                    steps.append((pieces["down"][hf][g], fn))
            return steps

        def final_norm_store(b, c, prefetch_next):
            sumsq_rstd()
            for i in range(4):
                if i < 2:
                    stg, stgb = EE[i][:, :], EE_B[i]
                else:
                    stg, stgb = X[:, i, :], [X_B[i]]
                P.op(dve, lambda e, i=i, stg=stg: e.scalar_tensor_tensor(
                    out=stg, in0=X[:, i, :], scalar=rstd[:, i:i + 1], in1=gfin[:],
                    op0=ALU.mult, op1=ALU.mult), reads=[X_B[i], RSTD_B, GFIN_B], writes=stgb)
                r0 = c * CH + i * 128
                P.dma(pool, out_d[b, r0:r0 + 128, :], stg, reads=stgb, owner=stgb[0], is_store=True)
                if prefetch_next and i < 2:
                    r1 = (c + 1) * CH + i * 128
                    P.dma(pool, X[:, i, :], x_d[b, r1:r1 + 128, :], writes=[X_B[i]])
            if prefetch_next:
                for i in (2, 3):
                    r1 = (c + 1) * CH + i * 128
                    P.dma(pool, X[:, i, :], x_d[b, r1:r1 + 128, :], writes=[X_B[i]])

        for b in range(NSEQ):
            steps = [(None, lambda w, wb: (load_x(b, 0), norm_to_hT(b, 0, 0)))]
            steps += q_steps("qsb") + k_steps("ksb", 0) + v_steps("vsb", 0)
            run_steps(steps)
            for c in range(NCH):
                qt0 = QT0 if c % 2 == 0 else 0
                qt0n = QT0 if (c + 1) % 2 == 0 else 0
                bg = make_bg(b, c + 1, qt0n) if c + 1 < NCH else None
                sb_attention(c, qt0, bg)
            emit_casts(10000)
            build_gate_bc(b)
            all_steps = []
            for c in range(NCH):
                if c == 0:
                    steps = [(None, lambda w, wb, c=c: (load_x(b, c), norm_to_hT(b, 0, 0)))]
                else:
                    steps = []
                steps += q_steps("qfx") + k_steps("kfx", c) + v_steps("vfx", c)
                th = fox_prep_thunks(c)
                gs = gate_steps()
                for gi in range(8):
                    if gi < len(th):
                        steps.append((None, th[gi]))
                    steps.append(gs[gi])
                steps += [(None, lambda w, wb, c=c: fox_attention(c))]
                steps += merge_steps(c) + out_steps()
                steps += [(None, lambda w, wb: norm_to_hT(b, 1, 3))]
                steps += gu_steps() + down_steps(make_pnorm(b, c + 1) if c + 1 < NCH else {})
                steps += [(None, lambda w, wb, c=c: final_norm_store(b, c, c + 1 < NCH))]
                all_steps += steps
            run_steps(all_steps)

        P.wait_all(sp, X_B + EE_B[0] + EE_B[1])
        P.emit()
    return nc


_NC_CACHE = {}


def kernel(x, c, w_ada, b_ada, g_mix, w_in, b_forget, b_gate, w_branch_sb, w_branch_fox,
           w_out, g_ffn, w_ffn_gate, w_ffn_up, w_ffn_down, g_final):
    f = lambda a: np.ascontiguousarray(np.asarray(a, dtype=np.float32))
    x = f(x)
    c = f(c)
    if "nc" not in _NC_CACHE:
        _NC_CACHE["nc"] = build_nc()
    nc = _NC_CACHE["nc"]

    def featT(v, nch):
        return f(np.asarray(v, dtype=np.float32).reshape(nch, 128).T)

    shared = {
        "w_ada": f(w_ada[0]),
        "b_adaT": featT(b_ada[0], 48),
        "g_mixT": featT(g_mix[0], 8),
        "g_ffnT": featT(g_ffn[0], 8),
        "gfin_bc": f(np.broadcast_to(np.asarray(g_final, dtype=np.float32)[None, :], (128, D))),
        "w_in": f(w_in[0]),
        "bfor_bc": f(np.broadcast_to(np.asarray(b_forget[0], dtype=np.float32)[None, :], (128, 8))),
        "b_gateT": featT(b_gate[0], 16),
        "w_bsb": f(w_branch_sb[0]),
        "w_bfx": f(w_branch_fox[0]),
        "w_out": f(w_out[0]),
        "w_fg": f(w_ffn_gate[0]),
        "w_fu": f(w_ffn_up[0]),
        "w_fd": f(w_ffn_down[0]),
    }
    in_maps = []
    for i in range(NCORES):
        m = dict(shared)
        m["x"] = x[NSEQ * i:NSEQ * (i + 1)]
        cc = c[NSEQ * i:NSEQ * (i + 1)]
        m["cT"] = f(cc.reshape(NSEQ, 8, 128).transpose(2, 1, 0))
        in_maps.append(m)
    res = run_bass_kernel_spmd(nc, in_maps, core_ids=list(range(NCORES)))
    out = np.concatenate([np.asarray(r["out"]) for r in res.results], axis=0)
    return out.astype(np.float32, copy=False)
```
